# Optimizing a Trainium2 kernel written in Bass

```python
import jax, jax.numpy as jnp
from jax import lax
import numpy as np

D_MODEL = 2048
BATCH = 16
SEQ = 256
DEPTH = 1
DEC_BATCH = 4
DEC_SEQ = 1024
PAST_LEN = 512

GRID_W = 64
HEAD_SIZE = 64
D_A = 2048
N_HEADS_A = D_A // HEAD_SIZE
D_B = 2048
R_DECAY = 96
R_ICLR = 96
N_DIR = 2
RMS_EPS = 1e-6
GN_EPS = HEAD_SIZE * 1e-5
N_SHIFTED = 3 * D_A + N_DIR * (R_DECAY + R_ICLR)
D_IN = N_SHIFTED + D_A + 4 * D_B + 2 * D_MODEL

kernel_name = "hybrid_rwkv7_shortconv_diffusion_step"


def _rmsnorm(x, g):
    xf = x.astype(jnp.float32)
    xf = xf * lax.rsqrt(jnp.mean(xf * xf, axis=-1, keepdims=True) + RMS_EPS)
    return (xf * g.astype(jnp.float32)).astype(x.dtype)


def _token_shift(z, mu_prev, mu_next):
    zp = jnp.pad(z[:, :-1], ((0, 0), (1, 0), (0, 0)))
    zn = jnp.pad(z[:, 1:], ((0, 0), (0, 1), (0, 0)))
    return z + mu_prev * (zp - z) + mu_next * (zn - z)


def _conv3_rows(u, w, rows):
    b, t, ch = u.shape
    ur = u.reshape(b, rows, t // rows, ch)
    up = jnp.pad(ur, ((0, 0), (0, 0), (1, 1), (0, 0)))
    out = w[0] * up[:, :, :-2] + w[1] * up[:, :, 1:-1] + w[2] * up[:, :, 2:]
    return out.reshape(b, t, ch)


def _heads(u):
    return u.reshape(u.shape[:-1] + (N_HEADS_A, HEAD_SIZE))


def _wkv_bidir(S0, r, decay, k, v, a_vec, b_vec):
    def orient(t):
        return jnp.stack([t[0], jnp.flip(t[1], axis=1)], axis=0)

    xs = tuple(jnp.moveaxis(orient(t), 2, 0) for t in (r, decay, k, v, a_vec, b_vec))

    def step(S, inp):
        r_t, w_t, k_t, v_t, a_t, b_t = inp
        sa = jnp.einsum('dbhij,dbhj->dbhi', S, a_t)
        S = S * w_t[..., None, :] + sa[..., :, None] * b_t[..., None, :] + v_t[..., :, None] * k_t[..., None, :]
        y = jnp.einsum('dbhij,dbhj->dbhi', S, r_t)
        return S, y

    S_fin, ys = lax.scan(step, S0, xs)
    return S_fin, orient(jnp.moveaxis(ys, 0, 2))


def _mixer(h, S0, rows, w_in, mu_prev, mu_next, w0, w2, a0, a2, k_k, k_a, r_k, lnx_g, lnx_b,
           conv_w, w_out_a, w_out_b, w_o):
    f32 = jnp.float32
    bsz, t, _ = h.shape
    z = h @ w_in
    zs = _token_shift(z[..., :N_SHIFTED], mu_prev, mu_next).astype(f32)
    r = zs[..., :D_A]
    k = zs[..., D_A:2 * D_A]
    v = zs[..., 2 * D_A:3 * D_A]
    o_w = 3 * D_A + N_DIR * R_DECAY
    wd = zs[..., 3 * D_A:o_w].reshape(bsz, t, N_DIR, R_DECAY)
    ad = zs[..., o_w:].reshape(bsz, t, N_DIR, R_ICLR)
    g_a, b_gate, c_gate, x_b, g_b, m_a, m_b = jnp.split(
        z[..., N_SHIFTED:],
        [D_A, D_A + D_B, D_A + 2 * D_B, D_A + 3 * D_B, D_A + 4 * D_B, D_A + 4 * D_B + D_MODEL], axis=-1)

    w_lin = w0[:, None, None, :] + jnp.einsum('btdr,drc->dbtc', jnp.tanh(wd), w2)
    decay = jnp.exp(-jnp.exp(-jax.nn.softplus(-w_lin) - 0.5))
    a = jax.nn.sigmoid(a0[:, None, None, :] + jnp.einsum('btdr,drc->dbtc', ad, a2))
    kk = _heads(k * k_k)
    kk = kk / jnp.maximum(jnp.sqrt(jnp.sum(kk * kk, axis=-1, keepdims=True)), 1e-12)
    kk = kk.reshape(bsz, t, D_A)
    k_dir = k[None] * (1.0 + (a - 1.0) * k_a)
    r2 = jnp.broadcast_to(r[None], (N_DIR,) + r.shape)
    v2 = jnp.broadcast_to(v[None], (N_DIR,) + v.shape)
    kk2 = jnp.broadcast_to(kk[None], (N_DIR,) + kk.shape)
    S_fin, ys = _wkv_bidir(S0, _heads(r2), _heads(decay), _heads(k_dir), _heads(v2),
                           _heads(-kk2), _heads(kk2 * a))
    y = ys[0] + ys[1]
    mu = jnp.mean(y, axis=-1, keepdims=True)
    var = jnp.mean(jnp.square(y - mu), axis=-1, keepdims=True)
    y = ((y - mu) * lax.rsqrt(var + GN_EPS)).reshape(bsz, t, D_A) * lnx_g + lnx_b
    bonus = jnp.sum(_heads(r2 * k_dir) * r_k, axis=-1, keepdims=True) * _heads(v2)
    y = y + jnp.sum(bonus, axis=0).reshape(bsz, t, D_A)
    y_a = (y.astype(h.dtype) * jax.nn.silu(g_a)) @ w_out_a

    u = _conv3_rows(c_gate * x_b, conv_w, rows)
    y_b = (b_gate * u * jax.nn.silu(g_b)) @ w_out_b

    merged = jax.nn.sigmoid(m_a) * y_a + jax.nn.sigmoid(m_b) * y_b
    return merged @ w_o, S_fin


def _block(x, mod, S0, rows, norm_g, *mixer_w):
    shift, scale, gate = jnp.split(mod, 3, axis=-1)
    h = _rmsnorm(x, norm_g) * (1.0 + scale) + shift
    out, S_fin = _mixer(h, S0, rows, *mixer_w)
    return x + gate * out, S_fin


def setup_inputs(seed: int = 0) -> dict:
    key = jax.random.key(seed)
    ks = jax.random.split(key, 26)
    f32 = jnp.float32

    def nrm(k, shape, s):
        return s * jax.random.normal(k, shape, f32)

    def uni(k, shape, lo, hi):
        return jax.random.uniform(k, shape, f32, lo, hi)

    L, H, N = DEPTH, N_HEADS_A, HEAD_SIZE
    return {
        "x_prompt": nrm(ks[0], (BATCH, SEQ, D_MODEL), 1.0),
        "x_sample": nrm(ks[1], (DEC_BATCH, DEC_SEQ, D_MODEL), 1.0),
        "state_wkv_fwd": nrm(ks[2], (DEC_BATCH, L, H, N, N), 0.5),
        "state_wkv_bwd": nrm(ks[3], (DEC_BATCH, L, H, N, N), 0.5),
        "c": nrm(ks[4], (DEC_BATCH, D_MODEL), 1.0),
        "c_ctx": nrm(ks[5], (D_MODEL,), 1.0),
        "ada_w": nrm(ks[6], (L, D_MODEL, 3 * D_MODEL), 0.5 * D_MODEL ** -0.5),
        "ada_b": nrm(ks[7], (L, 3 * D_MODEL), 0.02),
        "norm_g": 1.0 + nrm(ks[8], (L, D_MODEL), 0.02),
        "w_in": nrm(ks[9], (L, D_MODEL, D_IN), D_MODEL ** -0.5),
        "mu_prev": uni(ks[10], (L, N_SHIFTED), 0.0, 0.5),
        "mu_next": uni(ks[11], (L, N_SHIFTED), 0.0, 0.5),
        "w0": uni(ks[12], (L, N_DIR, D_A), -4.0, -0.5),
        "w2": nrm(ks[13], (L, N_DIR, R_DECAY, D_A), 0.3 * R_DECAY ** -0.5),
        "a0": nrm(ks[14], (L, N_DIR, D_A), 0.1),
        "a2": nrm(ks[15], (L, N_DIR, R_ICLR, D_A), R_ICLR ** -0.5),
        "k_k": 0.85 + nrm(ks[16], (L, D_A), 0.05),
        "k_a": 1.0 + nrm(ks[17], (L, D_A), 0.05),
        "r_k": nrm(ks[18], (L, H, N), 0.1),
        "lnx_g": 1.0 + nrm(ks[19], (L, D_A), 0.02),
        "lnx_b": nrm(ks[20], (L, D_A), 0.01),
        "conv_w": nrm(ks[21], (L, 3, D_B), 3 ** -0.5),
        "w_out_a": nrm(ks[22], (L, D_A, D_MODEL), D_A ** -0.5),
        "w_out_b": nrm(ks[23], (L, D_B, D_MODEL), D_B ** -0.5),
        "w_o": nrm(ks[24], (L, D_MODEL, D_MODEL), D_MODEL ** -0.5),
        "final_g": 1.0 + nrm(ks[25], (D_MODEL,), 0.02),
    }


def reference(x_prompt, x_sample, state_wkv_fwd, state_wkv_bwd, c, c_ctx, ada_w, ada_b, norm_g, w_in,
              mu_prev, mu_next, w0, w2, a0, a2, k_k, k_a, r_k, lnx_g, lnx_b, conv_w, w_out_a, w_out_b,
              w_o, final_g):
    f32 = jnp.float32
    n_ctx_batch = x_prompt.shape[0]
    lat_rows = x_sample.shape[1] // GRID_W
    xp = x_prompt
    xl = x_sample
    fwd_states = []
    bwd_states = []
    for l in range(DEPTH):
        lw = (norm_g[l], w_in[l], mu_prev[l], mu_next[l], w0[l], w2[l], a0[l], a2[l], k_k[l], k_a[l],
              r_k[l], lnx_g[l], lnx_b[l], conv_w[l], w_out_a[l], w_out_b[l], w_o[l])
        mod_ctx = (jax.nn.silu(c_ctx) @ ada_w[l] + ada_b[l])[None, None, :]
        mod_lat = (jax.nn.silu(c) @ ada_w[l] + ada_b[l])[:, None, :]
        S0_ctx = jnp.zeros((N_DIR, n_ctx_batch, N_HEADS_A, HEAD_SIZE, HEAD_SIZE), f32)
        xp, S_ctx = _block(xp, mod_ctx, S0_ctx, 1, *lw)
        fwd_states.append(S_ctx[0])
        bwd_states.append(S_ctx[1])
        S0_lat = jnp.stack([state_wkv_fwd[:, l], state_wkv_bwd[:, l]], axis=0).astype(f32)
        xl, _ = _block(xl, mod_lat, S0_lat, lat_rows, *lw)
    y_prompt = _rmsnorm(xp, final_g)
    y_sample = _rmsnorm(xl, final_g)
    new_state_wkv_fwd = jnp.stack(fwd_states, axis=1).astype(x_prompt.dtype)
    new_state_wkv_bwd = jnp.stack(bwd_states, axis=1).astype(x_prompt.dtype)
    return (y_prompt, y_sample, new_state_wkv_fwd, new_state_wkv_bwd)
```

```python
import numpy as np
import concourse.bass as bass
import concourse.mybir as mybir
from concourse.bass_utils import run_bass_kernel_spmd
from contextlib import ExitStack

F32 = mybir.dt.float32
BF16 = mybir.dt.bfloat16
AF = mybir.ActivationFunctionType
ALU = mybir.AluOpType
AX = mybir.AxisListType

D = 2048
T = 1024
KC = 16
NH = 32
DIN = 20864
NSH = 6528
CE = float(np.exp(-0.5))
RMS_EPS = 1e-6
GN_EPS = 64 * 1e-5

_o = 0
COL = {}
for _n, _w in [("mup", 48), ("mun", 48), ("mupl", 4), ("munl", 4), ("w0", 32), ("a0", 32), ("kk", 16), ("ka", 16),
               ("rk", 16), ("lg", 16), ("lb", 16), ("cw", 48), ("ng", 16), ("abs", 16), ("abc", 16), ("flag", 1)]:
    COL[_n] = _o
    _o += _w
NCOLS = _o
_o = 0
DCOL = {}
for _n, _w in [("c0", 48), ("c0l", 4), ("nmup", 48), ("nmun", 48), ("nmupl", 4), ("nmunl", 4), ("omka", 16),
               ("cw0n", 16), ("cw2n", 16), ("omf", 1), ("shift", 16), ("s1", 16), ("scale", 16)]:
    DCOL[_n] = _o
    _o += _w
NDCOLS = _o


class Prog:
    def __init__(self, nc, stack):
        self.nc = nc
        self.stack = stack
        self.ops = []
        self.lastw = {}
        self.readers = {}
        self.epoch = []
        self.bar_start = 0
        self.muted = False
        import os as _os
        self.cutlevel = float(_os.environ.get('K_CUT1', '99'))

    def sb(self, name, shape, dt, stack=None):
        return (stack or self.stack).enter_context(self.nc.sbuf_tensor(name, list(shape), dt))

    def ps(self, name, shape, dt):
        return self.stack.enter_context(self.nc.psum_tensor(name, list(shape), dt))

    def capture(self):
        self._cap = []

    def end_capture(self):
        c, self._cap = self._cap, None
        return c

    def replay(self, *caps):
        caps = [list(c) for c in caps if c]
        while caps:
            for c in list(caps):
                eng, fn, reads, writes, dk = c.pop(0)
                self._add(eng, fn, reads, writes, dk)
                if not c:
                    caps.remove(c)

    def replay_spread(self, main, side):
        main, side = list(main), list(side)
        n, m = len(main), len(side)
        j = 0
        for i, (eng, fn, reads, writes, dk) in enumerate(main):
            self._add(eng, fn, reads, writes, dk)
            want = ((i + 1) * m) // max(n, 1)
            while j < want:
                e2, f2, r2, w2, d2 = side[j]
                self._add(e2, f2, r2, w2, d2)
                j += 1
        while j < m:
            e2, f2, r2, w2, d2 = side[j]
            self._add(e2, f2, r2, w2, d2)
            j += 1

    def cut(self, n):
        if n >= self.cutlevel:
            self.muted = True

    def _add(self, eng, fn, reads, writes, dma_key=None):
        if self.muted:
            return -1
        if getattr(self, '_cap', None) is not None:
            self._cap.append((eng, fn, tuple(reads), tuple(writes), dma_key))
            return -1
        idx = len(self.ops)
        deps = set(self.epoch)
        for r in reads:
            if r in self.lastw:
                deps.add(self.lastw[r])
        for w in writes:
            if w in self.lastw:
                deps.add(self.lastw[w])
            for x in self.readers.get(w, {}).values():
                deps.add(x)
        self.ops.append(dict(eng=eng, fn=fn, deps=deps, dma_key=dma_key))
        rk = eng if dma_key is None else ('dma', idx)
        for r in reads:
            self.readers.setdefault(r, {})[rk] = idx
        for w in writes:
            self.lastw[w] = idx
            self.readers[w] = {}
        return idx

    def pe(self, fn, reads=(), writes=()):
        return self._add('pe', fn, reads, writes)

    def act(self, fn, reads=(), writes=()):
        return self._add('act', fn, reads, writes)

    def dve(self, fn, reads=(), writes=()):
        return self._add('dve', fn, reads, writes)

    def pool(self, fn, reads=(), writes=()):
        return self._add('pool', fn, reads, writes)

    def dma(self, queue, fn, reads=(), writes=(), key=None):
        assert key is not None
        return self._add(queue, fn, reads, writes, dma_key=key)

    def barrier(self, scr):
        nc = self.nc
        prev = set()
        last = {}
        for i in range(self.bar_start, len(self.ops)):
            op = self.ops[i]
            if op['dma_key'] is not None:
                prev.add(i)
            else:
                last[op['eng']] = i
        prev |= set(last.values())
        prev |= set(self.epoch)
        ids = []
        for e, fn in [('act', lambda: nc.scalar.copy(out=scr[0:1, 0:1], in_=scr[0:1, 4:5])),
                      ('dve', lambda: nc.vector.tensor_copy(out=scr[0:1, 1:2], in_=scr[0:1, 4:5])),
                      ('pool', lambda: nc.gpsimd.tensor_copy(out=scr[0:1, 2:3], in_=scr[0:1, 4:5]))]:
            idx = len(self.ops)
            self.ops.append(dict(eng=e, fn=fn, deps=set(prev), dma_key=None))
            ids.append(idx)
        self.epoch = ids
        self.bar_start = len(self.ops)
        self.lastw = {}
        self.readers = {}

    def emit(self):
        nc = self.nc
        ops = self.ops
        engs = ['pe', 'act', 'dve', 'pool', 'sp']
        need = set()
        for i, op in enumerate(ops):
            for d in op['deps']:
                po = ops[d]
                if po['dma_key'] is None and op['dma_key'] is None and po['eng'] == 'pe' and op['eng'] == 'pe':
                    continue
                need.add(d)
        ROT = 10 ** 9
        DROT = 10 ** 9
        sems = {e: [] for e in engs}
        dsems = {}
        dcnt = {}
        dfinal = []
        cnt = {e: 0 for e in engs}
        sig = {}
        nsem = [0]

        def newsem():
            nsem[0] += 1
            return self.stack.enter_context(nc.semaphore("q_%d" % nsem[0]))
        for i, op in enumerate(ops):
            if op['dma_key'] is not None:
                k = op['dma_key']
                u = dcnt.get(k, 0)
                if u % DROT == 0:
                    dsems[k] = newsem()
                dcnt[k] = u + 1
                v = (u % DROT + 1) * 16
                sig[i] = (dsems[k], v, 16)
                if v == 16:
                    dfinal.append([dsems[k], v])
                else:
                    dfinal[[id(x[0]) for x in dfinal].index(id(dsems[k]))][1] = v
            elif i in need:
                e = op['eng']
                if cnt[e] % ROT == 0:
                    sems[e].append(newsem())
                cnt[e] += 1
                sig[i] = (sems[e][-1], (cnt[e] - 1) % ROT + 1, 1)
        per = {e: [] for e in engs}
        for i, op in enumerate(ops):
            per[op['eng']].append(i)
        handles = {'pe': nc.tensor, 'act': nc.scalar, 'dve': nc.vector, 'pool': nc.gpsimd, 'sp': nc.sync}

        def run(e):
            h = handles[e]
            waited = {}
            for i in per[e]:
                op = ops[i]
                w = {}
                for d in op['deps']:
                    if d not in sig:
                        continue
                    s, v, _ = sig[d]
                    key = id(s)
                    if waited.get(key, 0) >= v:
                        continue
                    if key not in w or w[key][1] < v:
                        w[key] = (s, v)
                for key, (s, v) in w.items():
                    h.wait_ge(s, v)
                    waited[key] = v
                ins = op['fn']()
                if i in sig:
                    s, v, inc = sig[i]
                    ins.then_inc(s, inc)
            if e == 'sp':
                for s_, v_ in dfinal:
                    h.wait_ge(s_, v_)

        with nc.Block() as block:
            @block.sync
            def _(sync):
                run('sp')

            @block.tensor
            def _(t):
                run('pe')

            @block.scalar
            def _(t):
                run('act')

            @block.vector
            def _(t):
                run('dve')

            @block.gpsimd
            def _(t):
                run('pool')
        self.stats = dict(n_ops=len(ops), per={e: len(per[e]) for e in engs}, signals=dict(cnt), nsem=nsem[0])


class _Stop(Exception):
    pass


def build_program(dbg=(), upto=99, npairs=16):
    import os as _os0
    R_ = (lambda ap: ap.bitcast(mybir.dt.float32r)) if _os0.environ.get('K_F32R') else (lambda ap: ap)
    nc = bass.Bass("TRN2", target_bir_lowering=False)

    def din(name, shape):
        return nc.dram_tensor(name, list(shape), F32, kind="ExternalInput").ap()

    def dout(name, shape):
        return nc.dram_tensor(name, list(shape), F32, kind="ExternalOutput").ap()

    x_d = din("x", [T, D])
    cvec_d = din("cvec", [128, KC])
    cols_d = din("cols", [128, NCOLS])
    st_d = din("st", [2, NH, 64, 64])
    ident_d = din("ident", [128, 128])
    bones_d = din("bones", [128, 128])
    mask4_d = din("mask4", [2, 128, 512])
    maskP_d = din("maskP", [2, 128, 512])
    cm_d = din("cm", [128, T])
    rows_d = din("rows", [128, 2, D])
    adaw_d = din("ada_w", [D, 3 * D])
    win_d = din("w_in", [D, DIN])
    w2_d = din("w2", [2, 96, D])
    a2_d = din("a2", [2, 96, D])
    woa_d = din("w_out_a", [D, D])
    wob_d = din("w_out_b", [D, D])
    wo_d = din("w_o", [D, D])
    y_d = dout("y", [T, D])
    sto_d = dout("sto", [2, 4, 16, 128, 64])
    dbg_t = {n: dout("dbg_" + n, shp) for n, shp in dbg}

    winv = win_d.rearrange("(kc p) f -> p kc f", p=128)
    adav = adaw_d.rearrange("(kc p) f -> p kc f", p=128)
    woav = woa_d.rearrange("(kc p) f -> p kc f", p=128)
    wobv = wob_d.rearrange("(kc p) f -> p kc f", p=128)
    wov = wo_d.rearrange("(kc p) f -> p kc f", p=128)

    with ExitStack() as st:
        P = Prog(nc, st)
        ARENA = 212800
        big = st.enter_context(nc.sbuf_tensor("big", [128, ARENA], mybir.dt.uint8))

        class Arena:
            def __init__(self, off):
                self.off = off

            def at(self, shape, dt, off):
                esz = 4 if dt == F32 else 2
                n = int(np.prod(shape[1:])) * esz
                assert off + n <= ARENA, (off, n, ARENA)
                v = big[:, off:off + n].bitcast(dt)
                if len(shape) == 3:
                    v = v.rearrange("p (a b) -> p a b", a=shape[1])
                elif len(shape) == 4:
                    v = v.rearrange("p (a b c) -> p a b c", a=shape[1], b=shape[2])
                return v, n

        def sb(name, shape, dt, ar):
            v, n = ar.at(shape, dt, ar.off)
            ar.off += (n + 63) // 64 * 64
            return v

        AR = Arena(0)
        cols = sb("cols", [128, NCOLS], F32, AR)
        dcols = sb("dcols", [128, NDCOLS], F32, AR)
        scr = sb("scr", [128, 8], F32, AR)
        sc2 = sb("sc2", [128, KC, 2], F32, AR)
        ident_f = sb("ident_f", [128, 128], F32, AR)
        ident_b = sb("ident_b", [128, 128], BF16, AR)
        bones = sb("bones", [128, 128], F32, AR)
        wslot = sb("wslot", [128, KC, 512], BF16, AR)
        WSL2_OFF = AR.off
        wsl2 = sb("wsl2", [128, KC, 512], BF16, AR)
        R_OFF = AR.off
        Rbf = sb("R", [128, KC, T], BF16, AR)
        HT_OFF = AR.off
        hT = sb("hT", [128, KC, T], BF16, AR)
        pb = [P.ps("pb%d" % i, [128, 512], F32) for i in range(8)]

        def C(name, j=0, n=1, np_=128):
            o = COL[name] + j
            return cols[0:np_, o:o + n]

        def DC(name, j=0, n=1, np_=128):
            o = DCOL[name] + j
            return dcols[0:np_, o:o + n]

        def dbg_dump(name, src_ap, reads):
            if name in dbg_t:
                P.dma('pool', lambda: nc.gpsimd.dma_start(out=dbg_t[name], in_=src_ap), reads=reads, key=('dbg', name))

        P.dma('sp', lambda: nc.sync.dma_start(out=cols[:], in_=cols_d), writes=['cols'], key='c_cols')
        P.dma('sp', lambda: nc.sync.dma_start(out=ident_f[:], in_=ident_d), writes=['ident_f'], key='c_idf')
        P.dma('sp', lambda: nc.sync.dma_start(out=bones[:], in_=bones_d), writes=['bones'], key='c_bones')
        P.dma('pool', lambda: nc.gpsimd.dma_start(out=ident_b[:], in_=ident_d), writes=['ident_b'], key='c_idb')
        P.pool(lambda: nc.gpsimd.memset(scr[:], 0.0), writes=['scr'])

        def dcol_ops():
            V = nc.vector
            rw = dict(reads=['cols'], writes=['dcols'])
            P.dve(lambda: V.tensor_tensor(out=DC("c0", 0, 48), in0=C("mup", 0, 48), in1=C("mun", 0, 48), op=ALU.add), **rw)
            P.dve(lambda: V.tensor_scalar(out=DC("c0", 0, 48), in0=DC("c0", 0, 48), scalar1=-1.0, scalar2=1.0, op0=ALU.mult, op1=ALU.add), reads=['dcols'], writes=['dcols'])
            P.dve(lambda: V.tensor_tensor(out=DC("c0l", 0, 4), in0=C("mupl", 0, 4), in1=C("munl", 0, 4), op=ALU.add), **rw)
            P.dve(lambda: V.tensor_scalar(out=DC("c0l", 0, 4), in0=DC("c0l", 0, 4), scalar1=-1.0, scalar2=1.0, op0=ALU.mult, op1=ALU.add), reads=['dcols'], writes=['dcols'])
            P.dve(lambda: V.tensor_scalar(out=DC("omf"), in0=C("flag"), scalar1=-1.0, scalar2=1.0, op0=ALU.mult, op1=ALU.add), **rw)
            for dn, sn, n in [("nmup", "mup", 48), ("nmun", "mun", 48), ("nmupl", "mupl", 4), ("nmunl", "munl", 4)]:
                P.dve(lambda dn=dn, sn=sn, n=n: V.tensor_scalar(out=DC(dn, 0, n), in0=C(sn, 0, n), scalar1=DC("omf"), scalar2=-1.0, op0=ALU.mult, op1=ALU.mult), reads=['cols', 'dcols'], writes=['dcols'])
            P.dve(lambda: V.tensor_scalar(out=DC("omka", 0, 16), in0=C("ka", 0, 16), scalar1=-1.0, scalar2=1.0, op0=ALU.mult, op1=ALU.add), **rw)
            P.dve(lambda: V.tensor_scalar(out=DC("cw0n", 0, 16), in0=C("cw", 0, 16), scalar1=DC("omf"), scalar2=None, op0=ALU.mult), reads=['cols', 'dcols'], writes=['dcols'])
            P.dve(lambda: V.tensor_scalar(out=DC("cw2n", 0, 16), in0=C("cw", 32, 16), scalar1=DC("omf"), scalar2=None, op0=ALU.mult), reads=['cols', 'dcols'], writes=['dcols'])
        dcol_ops()
        import os as _os
        CUT = int(_os.environ.get('K_CUT', '99'))

        if True:
            s0 = Arena(AR.off)
            cv = sb("cv", [128, KC], F32, s0)
            aslot = [Arena(0).at([128, KC, 512], F32, R_OFF)[0], sb("aslotB", [128, KC, 512], F32, s0)]
            xt = [sb("xt%d" % i, [128, D], F32, s0) for i in range(2)]
            xsq = sb("xsq", [128, D], F32, s0)
            xn = [sb("xn%d" % i, [128, D], BF16, s0) for i in range(2)]
            rst = sb("rst", [128, 8, 4], F32, s0)

            P.dma('sp', lambda: nc.sync.dma_start(out=cv[:], in_=cvec_d), writes=['cv'], key='c_cv')
            P.act(lambda: nc.scalar.activation(out=sc2[:, :, 0], in_=cv[:], func=AF.Silu), reads=['cv'], writes=['sc2a'])
            P.act(lambda: nc.scalar.copy(out=sc2[:, :, 1], in_=sc2[:, :, 0]), reads=['sc2a'], writes=['sc2b'])

            macc = sb("macc", [128, 64], F32, s0)
            ring = [aslot[0][:].rearrange("p k f -> p (k f)")[:, 0:4096], aslot[0][:].rearrange("p k f -> p (k f)")[:, 4096:8192],
                    aslot[1][:].rearrange("p k f -> p (k f)")[:, 0:4096], aslot[1][:].rearrange("p k f -> p (k f)")[:, 4096:8192]]
            for kc in range(KC if CUT >= 2 else 0):
                sl = ring[kc % 4]
                sk = ('aslot', kc % 4)
                P.dma('sp', lambda kc=kc, sl=sl: nc.sync.dma_start(out=sl, in_=adaw_d[kc * 128:(kc + 1) * 128, 0:4096]), writes=[sk], key=sk)
                pbk = pb[kc % 2]
                bk = 'pb%d' % (kc % 2)
                for col in range(32):
                    P.pe(lambda sl=sl, kc=kc, col=col, pbk=pbk: nc.tensor.matmul(pbk[:, 2 * col:2 * col + 2], lhsT=sl[:, col * 128:(col + 1) * 128], rhs=sc2[:, kc, :], start=True, stop=True),
                         reads=[sk, 'sc2a', 'sc2b'], writes=[bk])
                if kc == 0:
                    P.dve(lambda pbk=pbk: nc.vector.tensor_copy(out=macc[:], in_=pbk[:, 0:64]), reads=[bk], writes=['macc'])
                else:
                    P.dve(lambda pbk=pbk: nc.vector.tensor_tensor(out=macc[:], in0=pbk[:, 0:64], in1=macc[:], op=ALU.add), reads=[bk, 'macc'], writes=['macc'])
            modv = macc[:].rearrange("p (c two) -> p c two", two=2)
            P.dve(lambda: nc.vector.tensor_tensor(out=DC("shift", 0, 16), in0=modv[:, 0:16, 0], in1=C("abs", 0, 16), op=ALU.add), reads=['macc', 'cols'], writes=['dc_shift'])
            P.dve(lambda: nc.vector.tensor_tensor(out=DC("scale", 0, 16), in0=modv[:, 16:32, 0], in1=C("abc", 0, 16), op=ALU.add), reads=['macc', 'cols'], writes=['dc_scale'])
            P.dve(lambda: nc.vector.scalar_tensor_tensor(out=DC("s1", 0, 16), in0=DC("scale", 0, 16), scalar=1.0, in1=C("ng", 0, 16), op0=ALU.add, op1=ALU.mult), reads=['dc_scale', 'cols'], writes=['dc_s1'])

            for tt in range(8 if CUT >= 3 else 0):
                xb_ = xt[tt % 2]
                xk = ('xt', tt % 2)
                xnb = xn[tt % 2]
                nk = ('xn', tt % 2)
                P.dma('sp', lambda tt=tt, xb_=xb_: nc.sync.dma_start(out=xb_[:], in_=x_d[tt * 128:(tt + 1) * 128, :]), writes=[xk], key=xk)
                P.act(lambda xb_=xb_: nc.scalar.activation(out=xsq[:], in_=xb_[:], func=AF.Square), reads=[xk], writes=['xsq'])
                P.dve(lambda tt=tt: nc.vector.tensor_reduce(out=rst[:, tt, 0:1], in_=xsq[:], axis=AX.X, op=ALU.add), reads=['xsq'], writes=[('rst', tt)])
                P.dve(lambda tt=tt: nc.vector.tensor_scalar(out=rst[:, tt, 1:2], in0=rst[:, tt, 0:1], scalar1=1.0 / D, scalar2=RMS_EPS, op0=ALU.mult, op1=ALU.add), reads=[('rst', tt)], writes=[('rst', tt)])
                P.act(lambda tt=tt: nc.scalar.activation(out=rst[:, tt, 2:3], in_=rst[:, tt, 1:2], func=AF.Sqrt), reads=[('rst', tt)], writes=[('rst', tt)])
                P.dve(lambda tt=tt: nc.vector.reciprocal(out=rst[:, tt, 3:4], in_=rst[:, tt, 2:3]), reads=[('rst', tt)], writes=[('rst', tt)])
                P.act(lambda tt=tt, xb_=xb_, xnb=xnb: nc.scalar.activation(out=xnb[:], in_=xb_[:], func=AF.Copy, scale=rst[:, tt, 3:4]), reads=[xk, ('rst', tt)], writes=[nk])
                for half in range(2 if CUT >= 4 else 0):
                    bank = pb[2 + half]
                    bk = 'pb%d' % (2 + half)
                    bv = bank[:].bitcast(BF16)
                    for j in range(8):
                        kc = half * 8 + j
                        P.pe(lambda bv=bv, j=j, kc=kc, xnb=xnb: nc.tensor.transpose(bv[:, j * 128:(j + 1) * 128], xnb[:, kc * 128:(kc + 1) * 128], ident_b[:]),
                             reads=[nk, 'ident_b'], writes=[bk])
                    for j in range(8 if CUT >= 5 else 0):
                        kc = half * 8 + j
                        if True:
                            P.dve(lambda bv=bv, j=j, kc=kc, tt=tt: nc.vector.tensor_scalar(out=hT[:, kc, tt * 128:(tt + 1) * 128], in0=bv[:, j * 128:(j + 1) * 128], scalar1=DC("s1", kc), scalar2=DC("shift", kc), op0=ALU.mult, op1=ALU.add),
                                  reads=[bk, 'dc_s1', 'dc_shift'], writes=[('hT', kc)])
                        else:
                            P.act(lambda bv=bv, j=j, kc=kc, tt=tt: nc.scalar.activation(out=hT[:, kc, tt * 128:(tt + 1) * 128], in_=bv[:, j * 128:(j + 1) * 128], func=AF.Identity, scale=DC("s1", kc), bias=DC("shift", kc)),
                                  reads=[bk, 'dc_s1', 'dc_shift'], writes=[('hT', kc)])
            dbg_dump("hT", hT[:], [('hT', kc) for kc in range(KC)])
        P.barrier(scr)
        HT_ALL = [('hT', kc) for kc in range(KC)]
        UPTO = upto

        def load_w(slot, key, pieces):
            flat = []
            for (c0, n, src) in pieces:
                if len(src.shape) == 4:
                    gg = src.shape[2]
                    ff = src.shape[3]
                    for gi in range(gg):
                        flat.append((c0 + gi * ff, ff, src[:, :, gi, :]))
                else:
                    flat.append((c0, n, src))
            for i, (c0, n, src) in enumerate(flat):
                dst = slot[:, :, c0:c0 + n]
                P.dma('pool', lambda dst=dst, src=src: nc.gpsimd.dma_start(out=dst, in_=src), writes=[(key, i)], key=(key, i))
            return [(key, i) for i in range(len(flat))]

        def mm_group(bank, bkey, slot, wkeys, wc0, M, rhs_fn, rkeys, np_out=128):
            for kc in range(KC):
                P.pe(lambda kc=kc: nc.tensor.matmul(bank[0:np_out, :], lhsT=slot[:, kc, wc0:wc0 + M], rhs=rhs_fn(kc), start=(kc == 0), stop=(kc == KC - 1)),
                     reads=list(wkeys) + list(rkeys), writes=[bkey])

        def tshift_items(z, zk, out, ok, np_, c0, mup, mun, nmup, nmun):
            V = nc.vector
            return [
                lambda: P.dve(lambda: V.tensor_scalar(out=out[0:np_, :], in0=z[0:np_, :], scalar1=c0, scalar2=None, op0=ALU.mult), reads=[zk, 'dcols'], writes=[ok]),
                lambda: P.dve(lambda: V.scalar_tensor_tensor(out=out[0:np_, 1:T], in0=z[0:np_, 0:T - 1], scalar=mup, in1=out[0:np_, 1:T], op0=ALU.mult, op1=ALU.add), reads=[zk, 'cols'], writes=[ok]),
                lambda: P.dve(lambda: V.scalar_tensor_tensor(out=out[0:np_, 0:T - 1], in0=z[0:np_, 1:T], scalar=mun, in1=out[0:np_, 0:T - 1], op0=ALU.mult, op1=ALU.add), reads=[zk, 'cols'], writes=[ok]),
                lambda: P.dve(lambda: V.scalar_tensor_tensor(out=out[0:np_, 256:T:256], in0=z[0:np_, 255:T - 1:256], scalar=nmup, in1=out[0:np_, 256:T:256], op0=ALU.mult, op1=ALU.add), reads=[zk, 'dcols'], writes=[ok]),
                lambda: P.dve(lambda: V.scalar_tensor_tensor(out=out[0:np_, 255:T - 1:256], in0=z[0:np_, 256:T:256], scalar=nmun, in1=out[0:np_, 255:T - 1:256], op0=ALU.mult, op1=ALU.add), reads=[zk, 'dcols'], writes=[ok]),
            ]

        def tshift(*a):
            for it in tshift_items(*a):
                it()

        yA = sb("yA", [128, KC, T], BF16, AR)
        YA_END = AR.off
        if True:
            s1 = Arena(YA_END)
            rA = Arena(R_OFF)
            wA = Arena(WSL2_OFF)
            ATall = sb("ATall", [128, 16, 512], BF16, rA)
            rat = sb("rat", [128, 8, 256], BF16, rA)
            ktl = sb("ktl", [128, T], BF16, rA)
            btl = sb("btl", [128, T], BF16, rA)
            kpf = sb("kpf", [128, T], BF16, rA)
            bpf = sb("bpf", [128, T], BF16, rA)
            vbf = sb("vbf", [128, T], BF16, rA)
            sgbs = [sb("sgb0", [128, T], BF16, rA), sb("sgb1", [128, T], BF16, s1)]
            assert rA.off <= R_OFF + 32768
            fsig = sb("f_sig", [128, T], F32, wA)
            fa = sb("f_a", [128, T], F32, wA)
            ft1 = sb("f_t1", [128, T], F32, wA)
            ft2 = sb("f_t2", [128, T], F32, wA)
            assert wA.off <= WSL2_OFF + 16384
            lora = sb("lora", [128, 4, T], BF16, s1)
            lw = sb("lw", [128, 4, 128], BF16, s1)
            mask4 = sb("mask4", [128, 2, 512], BF16, s1)
            maskP = sb("maskP", [128, 2, 512], BF16, s1)
            cm = sb("cm", [128, T], F32, s1)
            fr = sb("f_r", [128, T], F32, s1)
            fk = sb("f_k", [128, T], F32, s1)
            fvs = [sb("f_v%d" % i, [128, T], F32, s1) for i in range(2)]
            fs = sb("f_s", [128, T], F32, s1)
            fL = sb("f_L", [128, T], F32, s1)
            fkk = sb("f_kk", [128, T], F32, s1)
            ysb = sb("ysb", [128, T], F32, s1)
            VT = sb("VT", [128, 8, 128], BF16, s1)
            KT = sb("KT", [128, 8, 128], BF16, s1)
            BT = sb("BT", [128, 8, 128], BF16, s1)
            Pq = [sb("Pq%d" % i, [128, 4, 128], F32, s1) for i in range(2)]
            Qq = [sb("Qq%d" % i, [128, 4, 128], F32, s1) for i in range(2)]
            Nn = sb("Nn", [128, 4, 128], F32, s1)
            H32d = sb("H32d", [128, 128], F32, s1)
            Hbd = sb("Hbd", [128, 128], BF16, s1)
            VTz = sb("VTz", [128, 2, 8, 128], BF16, s1)
            Uz = sb("Uz", [128, 2, 128], BF16, s1)
            Wsb = sb("Wsb", [128, 128], BF16, s1)
            Usb = sb("Usb", [128, 128], BF16, s1)
            stin = sb("stin", [128, 128], F32, s1)
            stst = sb("stst", [128, 2, 4, 64], F32, s1)
            gC = sb("gC", [128, 2, 8], F32, s1)
            LsC = sb("LsC", [128, 8], F32, s1)
            nLsC = sb("nLsC", [128, 8], F32, s1)
            print("phase1 arena end", s1.off, "of", ARENA)

            P.dma('pool', lambda: nc.gpsimd.dma_start(out=mask4[:], in_=mask4_d.rearrange("d p f -> p d f")), writes=['mask4'], key='c_m4')
            P.dma('pool', lambda: nc.gpsimd.dma_start(out=maskP[:], in_=maskP_d.rearrange("d p f -> p d f")), writes=['maskP'], key='c_mP')
            P.dma('sp', lambda: nc.sync.dma_start(out=cm[:], in_=cm_d), writes=['cm'], key='c_cm')
            P.pool(lambda: nc.gpsimd.memset(stin[:], 0.0), writes=['stin0'])
            P.pool(lambda: nc.gpsimd.memset(VTz[:].rearrange("p e c f -> p (e c f)"), 0.0), writes=['VTz'])
            P.pool(lambda: nc.gpsimd.memset(Uz[:].rearrange("p e f -> p (e f)"), 0.0), writes=['Uz'])
            _uzf = Uz[:].rearrange("p e f -> p (e f)")
            Uz_diag = bass.AP(tensor=_uzf.tensor, offset=_uzf.offset, ap=[list(_uzf.ap[0]), [192, 2], [1, 64]])

            wk = load_w(wslot, 'wslot', [(0, 384, winv[:, :, 6144:6528])])
            for i in range(4 if upto >= 1 else 0):
                for tb in range(2):
                    bank = pb[tb]
                    mm_group(bank, 'pb%d' % tb, wslot, wk, 96 * i, 96, lambda kc, tb=tb: hT[:, kc, tb * 512:(tb + 1) * 512], HT_ALL, np_out=96)
                    P.act(lambda tb=tb, bank=bank: nc.scalar.copy(out=fL[0:96, tb * 512:(tb + 1) * 512], in_=bank[0:96, :]), reads=['pb%d' % tb], writes=['fL'])
                tshift(fL, 'fL', ft1, 'ft1', 96, DC("c0l", i, 1, 96), C("mupl", i, 1, 96), C("munl", i, 1, 96), DC("nmupl", i, 1, 96), DC("nmunl", i, 1, 96))
                P.act(lambda i=i: nc.scalar.activation(out=lora[0:96, i, :], in_=ft1[0:96, :], func=(AF.Tanh if i < 2 else AF.Copy)), reads=['ft1'], writes=[('lora', i)])

            wks = {}

            def pair_w(pp):
                return load_w(wslot, 'wslot', [
                    (0, 384, winv[:, :, 0:6144].rearrange("p k (g f) -> p k g f", g=3)[:, :, :, pp * 128:(pp + 1) * 128]),
                    (384, 128, winv[:, :, NSH + pp * 128:NSH + (pp + 1) * 128])])

            def main_items(pp):
                A_ = nc.scalar
                wk_ = wks[pp]
                dsts = [(fr, 'fr'), (fk, 'fk'), (fvs[pp % 2], 'fv%d' % (pp % 2)), (None, None)]

                def mm(fi, tb):
                    bank = pb[(fi % 2) * 2 + tb]
                    bk = 'pb%d' % ((fi % 2) * 2 + tb)
                    return lambda: mm_group(bank, bk, wslot, wk_, fi * 128, 128, lambda kc: hT[:, kc, tb * 512:(tb + 1) * 512], HT_ALL)

                def ev(fi, tb):
                    bank = pb[(fi % 2) * 2 + tb]
                    bk = 'pb%d' % ((fi % 2) * 2 + tb)
                    if fi < 3:
                        return lambda: P.act(lambda: A_.copy(out=fL[:, tb * 512:(tb + 1) * 512], in_=bank[:]), reads=[bk], writes=['fL'])
                    sg_, sk_ = sgbs[pp % 2], 'sgb%d' % (pp % 2)
                    return lambda: P.act(lambda: A_.activation(out=sg_[:, tb * 512:(tb + 1) * 512], in_=bank[:], func=AF.Silu), reads=[bk], writes=[sk_])

                def ts(fi):
                    ti = fi * 16 + pp
                    return tshift_items(fL, 'fL', dsts[fi][0], dsts[fi][1], 128, DC("c0", ti), C("mup", ti), C("mun", ti), DC("nmup", ti), DC("nmun", ti))
                t0_, t1_, t2_ = ts(0), ts(1), ts(2)
                items = [[mm(0, 0)],
                         [mm(0, 1), ev(0, 0)],
                         [mm(1, 0), ev(0, 1), t0_[0], t0_[1]],
                         [mm(1, 1), t0_[2], t0_[3], t0_[4]],
                         [mm(2, 0), ev(1, 0), ev(1, 1), t1_[0]],
                         [mm(2, 1), t1_[1], t1_[2], t1_[3], t1_[4]],
                         [mm(3, 0), ev(2, 0), ev(2, 1), t2_[0], t2_[1]],
                         [mm(3, 1), t2_[2], t2_[3], t2_[4], ev(3, 0)],
                         [ev(3, 1)]]
                return items

            def emit_shared(pp):
                V = nc.vector
                A = nc.scalar
                fvp, fvk = fvs[pp % 2], 'fv%d' % (pp % 2)
                P.act(lambda fvp=fvp: A.copy(out=vbf[:], in_=fvp[:]), reads=[fvk], writes=['vbf'])
                P.act(lambda p=pp: A.activation(out=fkk[:], in_=fk[:], func=AF.Copy, scale=C("kk", p)), reads=['fk', 'cols'], writes=['fkk'])
                P.act(lambda: A.activation(out=fL[:], in_=fkk[:], func=AF.Square), reads=['fkk'], writes=['fL'])
                for tb in range(2):
                    P.pe(lambda tb=tb: nc.tensor.matmul(pb[6 + tb][:], lhsT=bones[:], rhs=fL[:, tb * 512:(tb + 1) * 512], start=True, stop=True), reads=['bones', 'fL'], writes=['pb%d' % (6 + tb)])
                    P.act(lambda tb=tb: A.activation(out=fa[:, tb * 512:(tb + 1) * 512], in_=pb[6 + tb][:], func=AF.Sqrt), reads=['pb%d' % (6 + tb)], writes=['fa'])
                P.dve(lambda: V.tensor_scalar(out=fa[:], in0=fa[:], scalar1=1e-12, scalar2=None, op0=ALU.max), reads=['fa'], writes=['fa'])
                P.dve(lambda: V.reciprocal(out=fa[:], in_=fa[:]), reads=['fa'], writes=['fa'])
                P.dve(lambda: V.tensor_tensor(out=fkk[:], in0=fkk[:], in1=fa[:], op=ALU.mult), reads=['fkk', 'fa'], writes=['fkk'])
                vbank = pb[4][:].bitcast(BF16)
                for c in range(8):
                    P.pe(lambda c=c: nc.tensor.transpose(vbank[:, c * 128:(c + 1) * 128], vbf[:, c * 128:(c + 1) * 128], ident_b[:]), reads=['vbf', 'ident_b'], writes=['pb4'])
                P.act(lambda: A.copy(out=VT[:].rearrange("p c f -> p (c f)"), in_=vbank[:]), reads=['pb4'], writes=['VT', 'lk4'])
                vb3 = vbank[:].rearrange("p (c f) -> p c f", f=128)
                P.dve(lambda vb3=vb3: V.tensor_copy(out=VTz[:, 0, :, 0:64], in_=vb3[:, :, 0:64]), reads=['pb4'], writes=['VTz', 'lk4'])
                P.act(lambda vb3=vb3: A.copy(out=VTz[:, 1, :, 64:128], in_=vb3[:, :, 64:128]), reads=['pb4'], writes=['VTz', 'lk4'])
                if pp == 0:
                    dbg_dump("fr", fr[:], ['fr'])
                    dbg_dump("fk", fk[:], ['fk'])
                    dbg_dump("fv", fvp[:], [fvk])
                    dbg_dump("fkk", fkk[:], ['fkk'])


            for p in range(npairs if upto >= 1 else 0):
                V = nc.vector
                G = nc.gpsimd
                A = nc.scalar
                fvp, fvk = fvs[p % 2], 'fv%d' % (p % 2)
                sgp, sgk = sgbs[p % 2], 'sgb%d' % (p % 2)
                if p == 0:
                    wks[0] = pair_w(0)
                    for grp in main_items(0):
                        for it in grp:
                            it()
                    if npairs > 1:
                        wks[1] = pair_w(1)
                P.dma('pool', lambda p=p: nc.gpsimd.dma_start(out=lw[0:96, 0:2, :], in_=w2_d[:, :, p * 128:(p + 1) * 128].rearrange("d r f -> r d f")), writes=['lw0'], key='lw0')
                P.dma('pool', lambda p=p: nc.gpsimd.dma_start(out=lw[0:96, 2:4, :], in_=a2_d[:, :, p * 128:(p + 1) * 128].rearrange("d r f -> r d f")), writes=['lw1'], key='lw1')
                P.cut(1)
                if p == 0:
                    emit_shared(0)
                def prep_front(d, wb=(0, 1), ab=(2, 3)):
                    for tb in range(2):
                        P.pe(lambda tb=tb, d=d: nc.tensor.matmul(pb[wb[tb]][:], lhsT=lw[0:96, d, :], rhs=lora[0:96, d, tb * 512:(tb + 1) * 512], start=True, stop=True), reads=['lw0', ('lora', d)], writes=['pb%d' % wb[tb]])
                        P.act(lambda tb=tb, d=d, p=p: A.activation(out=fsig[:, tb * 512:(tb + 1) * 512], in_=pb[wb[tb]][:], func=AF.Sigmoid, bias=C("w0", d * 16 + p)), reads=['pb%d' % wb[tb], 'cols'], writes=['fsig'])
                        P.pe(lambda tb=tb, d=d: nc.tensor.matmul(pb[ab[tb]][:], lhsT=lw[0:96, 2 + d, :], rhs=lora[0:96, 2 + d, tb * 512:(tb + 1) * 512], start=True, stop=True), reads=['lw1', ('lora', 2 + d)], writes=['pb%d' % ab[tb]])
                        P.act(lambda tb=tb, d=d, p=p: A.activation(out=fa[:, tb * 512:(tb + 1) * 512], in_=pb[ab[tb]][:], func=AF.Sigmoid, bias=C("a0", d * 16 + p)), reads=['pb%d' % ab[tb], 'cols'], writes=['fa'])
                    P.cut(3)
                    if d == 0:
                        P.dve(lambda: V.tensor_tensor_scan(out=fL[:], data0=cm[:], data1=fsig[:], initial=0.0, op0=ALU.mult, op1=ALU.add), reads=['cm', 'fsig'], writes=['fL'])
                        endc = 127
                    else:
                        P.dve(lambda: V.tensor_tensor_scan(out=fL[:, ::-1], data0=cm[:], data1=fsig[:, ::-1], initial=0.0, op0=ALU.mult, op1=ALU.add), reads=['cm', 'fsig'], writes=['fL'])
                        endc = 0
                    fL3 = fL[:].rearrange("p (c t) -> p c t", t=128)
                    P.cut(4)
                    P.pool(lambda endc=endc: G.tensor_copy(out=LsC[:], in_=fL3[:, :, endc]), reads=['fL'], writes=['LsC'])
                    P.dve(lambda: V.tensor_scalar(out=nLsC[:], in0=LsC[:], scalar1=-CE, scalar2=None, op0=ALU.mult), reads=['LsC'], writes=['nLsC'])
                    P.act(lambda d=d: A.activation(out=gC[:, d, :], in_=LsC[:], func=AF.Exp, scale=-CE), reads=['LsC'], writes=[('gC', d)])
                    P.pool(lambda p=p: G.tensor_scalar(out=ft1[:], in0=fa[:], scalar1=C("ka", p), scalar2=DC("omka", p), op0=ALU.mult, op1=ALU.add), reads=['fa', 'cols', 'dcols'], writes=['ft1'])
                    P.pool(lambda: G.tensor_tensor(out=ft2[:], in0=fL[:], in1=fsig[:], op=ALU.subtract), reads=['fL', 'fsig'], writes=['ft2'])
                    P.dve(lambda: V.tensor_tensor(out=ft1[:], in0=ft1[:], in1=fk[:], op=ALU.mult), reads=['ft1', 'fk'], writes=['ft1'])
                    P.dve(lambda: V.tensor_tensor(out=fa[:], in0=fa[:], in1=fkk[:], op=ALU.mult), reads=['fa', 'fkk'], writes=['fa'])
                    if d == 0:
                        P.dve(lambda p=p: V.scalar_tensor_tensor(out=fs[:], in0=fr[:], scalar=C("rk", p), in1=ft1[:], op0=ALU.mult, op1=ALU.mult), reads=['fr', 'ft1', 'cols'], writes=['fs'])
                    else:
                        P.dve(lambda p=p: V.scalar_tensor_tensor(out=fsig[:], in0=fr[:], scalar=C("rk", p), in1=ft1[:], op0=ALU.mult, op1=ALU.mult), reads=['fr', 'ft1', 'cols', 'ft2'], writes=['fsig'])

                for d in range(2):
                    if d == 0:
                        prep_front(0)
                    fL3 = fL[:].rearrange("p (c t) -> p c t", t=128)
                    rat3 = rat[:]
                    fr3 = fr[:].rearrange("p (c t) -> p c t", t=128)
                    fkk3 = fkk[:].rearrange("p (c t) -> p c t", t=128)
                    P.act(lambda: A.activation(out=rat3[:, :, 0:128], in_=fL3, func=AF.Exp, scale=-CE), reads=['fL'], writes=['rat_r'])
                    P.act(lambda: A.activation(out=btl[:], in_=fL[:], func=AF.Exp, scale=CE), reads=['fL'], writes=['btl'])
                    P.act(lambda: A.activation(out=rat3[:, :, 128:256], in_=ft2[:].rearrange("p (c t) -> p c t", t=128), func=AF.Exp, scale=-CE), reads=['ft2'], writes=['rat_a'])
                    for c in range(8):
                        P.act(lambda c=c: A.activation(out=bpf[:, c * 128:(c + 1) * 128], in_=fL[:, c * 128:(c + 1) * 128], func=AF.Exp, scale=CE, bias=nLsC[:, c:c + 1]), reads=['fL', 'nLsC'], writes=['bpf'])
                    P.dve(lambda: V.tensor_tensor(out=rat3[:, :, 0:128], in0=rat3[:, :, 0:128], in1=fr3, op=ALU.mult), reads=['fr', 'rat_r'], writes=['rat_r'])
                    P.dve(lambda: V.tensor_tensor(out=ktl[:], in0=btl[:], in1=ft1[:], op=ALU.mult), reads=['ft1', 'btl'], writes=['ktl'])
                    P.dve(lambda: V.scalar_tensor_tensor(out=rat3[:, :, 128:256], in0=rat3[:, :, 128:256], scalar=-1.0, in1=fkk3, op0=ALU.mult, op1=ALU.mult), reads=['fkk', 'rat_a'], writes=['rat_a'])
                    P.dve(lambda: V.tensor_tensor(out=kpf[:], in0=bpf[:], in1=ft1[:], op=ALU.mult), reads=['ft1', 'bpf'], writes=['kpf'])
                    P.pool(lambda: G.tensor_tensor(out=btl[:], in0=btl[:], in1=fa[:], op=ALU.mult), reads=['fa', 'btl'], writes=['btl'])
                    P.pool(lambda: G.tensor_tensor(out=bpf[:], in0=bpf[:], in1=fa[:], op=ALU.mult), reads=['fa', 'bpf'], writes=['bpf'])
                    if d == 1:
                        P.pool(lambda: G.tensor_tensor(out=fs[:], in0=fs[:], in1=fsig[:], op=ALU.add), reads=['fs', 'fsig'], writes=['fs'])
                    P.cut(5)
                    if p == 0 and d == 0:
                        dbg_dump("rat", rat[:], ['rat_r', 'rat_a'])
                        dbg_dump("ktl", ktl[:], ['ktl'])
                        dbg_dump("KT", KT[:], ['KT'])
                    P.cut(6)
                    RAT = ['rat_r', 'rat_a']
                    P.capture()
                    for c in range(8):
                        for e in range(2):
                            inst = e * 8 + c
                            bank = pb[3 + inst % 3]
                            bk = 'pb%d' % (3 + inst % 3)
                            r0 = 64 * e
                            P.pe(lambda c=c, r0=r0, bank=bank: nc.tensor.matmul(bank[:, 0:256], lhsT=ktl[r0:r0 + 64, c * 128:(c + 1) * 128], rhs=rat[r0:r0 + 64, c, :], start=True, stop=True), reads=['ktl'] + RAT, writes=[bk])
                            P.pe(lambda c=c, r0=r0, bank=bank: nc.tensor.matmul(bank[:, 256:384], lhsT=btl[r0:r0 + 64, c * 128:(c + 1) * 128], rhs=rat[r0:r0 + 64, c, 0:128], start=True, stop=True), reads=['btl', 'rat_r'], writes=[bk])
                            P.act(lambda inst=inst, bank=bank: A.copy(out=ATall[:, inst, 0:384], in_=bank[:, 0:384]), reads=[bk], writes=[('AT', inst)])
                            P.dve(lambda inst=inst, d=d: V.tensor_tensor(out=ATall[:, inst, 0:384], in0=ATall[:, inst, 0:384], in1=mask4[:, d, 0:384], op=ALU.mult), reads=[('AT', inst), 'mask4'], writes=[('AT', inst)])
                    for (srcf, sk_, dstT, dk_, bi) in [(kpf, 'kpf', KT, 'KT', 3), (bpf, 'bpf', BT, 'BT', 4)]:
                        tbank = pb[bi][:].bitcast(BF16)
                        for c in range(8):
                            P.pe(lambda c=c, srcf=srcf, tbank=tbank: nc.tensor.transpose(tbank[:, c * 128:(c + 1) * 128], srcf[:, c * 128:(c + 1) * 128], ident_b[:]), reads=[sk_, 'ident_b'], writes=['pb%d' % bi])
                        P.act(lambda dstT=dstT, tbank=tbank: A.copy(out=dstT[:].rearrange("p c f -> p (c f)"), in_=tbank[:]), reads=['pb%d' % bi], writes=[dk_])
                    cap_a = P.end_capture()
                    P.capture()
                    P.cut(7)
                    for bt_ in range(4):
                        e_ = bt_ // 2
                        r0 = 64 * e_
                        cs = [(bt_ % 2) * 4 + i for i in range(4)]
                        insts = [e_ * 8 + c for c in cs]
                        for i, c in enumerate(cs):
                            P.pe(lambda c=c, i=i, r0=r0: nc.tensor.matmul(pb[6][:, i * 128:(i + 1) * 128], lhsT=rat[r0:r0 + 64, c, 128:256], rhs=btl[r0:r0 + 64, c * 128:(c + 1) * 128], start=True, stop=True),
                                 reads=['rat_a', 'btl'], writes=['pb6'])
                        for i, c in enumerate(cs):
                            P.pe(lambda c=c, i=i, r0=r0: nc.tensor.matmul(pb[7][:, i * 128:(i + 1) * 128], lhsT=btl[r0:r0 + 64, c * 128:(c + 1) * 128], rhs=rat[r0:r0 + 64, c, 128:256], start=True, stop=True),
                                 reads=['rat_a', 'btl'], writes=['pb7'])
                        P.dve(lambda d=d: V.tensor_tensor(out=Pq[0][:].rearrange("p i f -> p (i f)"), in0=pb[6][:], in1=maskP[:, d, :], op=ALU.mult), reads=['pb6', 'maskP'], writes=[('Pq', 0)])
                        P.dve(lambda d=d: V.tensor_tensor(out=Qq[0][:].rearrange("p i f -> p (i f)"), in0=pb[7][:], in1=maskP[:, 1 - d, :], op=ALU.mult), reads=['pb7', 'maskP'], writes=[('Qq', 0)])
                        P.pool(lambda: G.tensor_tensor(out=Nn[:], in0=Qq[0][:], in1=ident_f[:].unsqueeze(1).to_broadcast([128, 4, 128]), op=ALU.add), reads=[('Qq', 0), 'ident_f'], writes=['Nn'])
                        for lev in range(1, 7):
                            cur = (lev - 1) % 2
                            nxt = lev % 2
                            bX, bY, bZ = pb[0], pb[1], pb[2]
                            kX, kY, kZ = 'pb0', 'pb1', 'pb2'
                            for i in range(4):
                                P.pe(lambda bX=bX, i=i, cur=cur: nc.tensor.matmul(bX[:, i * 128:(i + 1) * 128], lhsT=R_(Qq[cur][:, i, :]), rhs=R_(Pq[cur][:, i, :]), start=True, stop=True), reads=[('Qq', cur), ('Pq', cur)], writes=[kX])
                            P.act(lambda bX=bX, nxt=nxt: A.copy(out=Pq[nxt][:].rearrange("p i f -> p (i f)"), in_=bX[:]), reads=[kX], writes=[('Pq', nxt)])
                            if lev < 6:
                                for i in range(4):
                                    P.pe(lambda bY=bY, i=i, cur=cur: nc.tensor.matmul(bY[:, i * 128:(i + 1) * 128], lhsT=R_(Pq[cur][:, i, :]), rhs=R_(Qq[cur][:, i, :]), start=True, stop=True), reads=[('Qq', cur), ('Pq', cur)], writes=[kY])
                                P.act(lambda bY=bY, nxt=nxt: A.copy(out=Qq[nxt][:].rearrange("p i f -> p (i f)"), in_=bY[:]), reads=[kY], writes=[('Qq', nxt)])
                            for i in range(4):
                                P.pe(lambda bZ=bZ, i=i, nxt=nxt: nc.tensor.matmul(bZ[:, i * 128:(i + 1) * 128], lhsT=R_(Pq[nxt][:, i, :]), rhs=R_(Nn[:, i, :]), start=True, stop=True), reads=[('Pq', nxt), 'Nn'], writes=[kZ])
                            if lev < 6:
                                P.dve(lambda bZ=bZ: V.tensor_tensor(out=Nn[:], in0=bZ[:].rearrange("p (i f) -> p i f", f=128), in1=Nn[:], op=ALU.add), reads=[kZ, 'Nn'], writes=['Nn'])
                            else:
                                i0 = insts[0]
                                P.dve(lambda bZ=bZ, i0=i0: V.tensor_tensor(out=ATall[:, i0:i0 + 4, 384:512], in0=bZ[:].rearrange("p (i f) -> p i f", f=128), in1=Nn[:], op=ALU.add),
                                      reads=[kZ, 'Nn'], writes=[('ATq', i_) for i_ in insts])
                    cap_i = P.end_capture()
                    side = list(cap_a)
                    if d == 0:
                        P.capture()
                        prep_front(1, wb=(6, 7), ab=(6, 7))
                        cap_f = P.end_capture()
                        side = []
                        ia, jf = list(cap_a), list(cap_f)
                        while ia or jf:
                            for _ in range(3):
                                if ia:
                                    side.append(ia.pop(0))
                            if jf:
                                side.append(jf.pop(0))
                    P.replay_spread(cap_i, side)
                    if p == 0 and d == 0:
                        dbg_dump("ATall", ATall[:], [('AT', i_) for i_ in range(16)] + [('ATq', i_) for i_ in range(16)])
                    P.cut(8)
                    for e in range(2):
                        P.dma('sp', lambda p=p, d=d, e=e: nc.sync.dma_start(out=stin[64 * e:64 * e + 64, 64 * e:64 * e + 64], in_=st_d[d, 2 * p + e]), reads=['stin0'], writes=[('stin', e)], key=('stin', e))
                    pw = pb[6][:, 0:128]
                    pu = pb[6][:, 128:256]
                    py = pb[7][:, 0:128]
                    ph0 = pb[7][:, 128:256]
                    P.pe(lambda: nc.tensor.transpose(ph0, stin[:], ident_f[:]), reads=[('stin', 0), ('stin', 1), 'stin0', 'ident_f'], writes=['pb7'])
                    P.dve(lambda: V.tensor_copy(out=H32d[:], in_=ph0), reads=['pb7'], writes=['H32'])
                    P.act(lambda: A.copy(out=Hbd[:], in_=H32d[:]), reads=['H32'], writes=['Hb'])
                    P.cut(9)
                    order = list(range(8)) if d == 0 else list(range(7, -1, -1))
                    pend = []
                    if d == 1 and p + 1 < npairs:
                        pend = main_items(p + 1)
                    if _os.environ.get('K_ORDER'):
                        order = [int(t_) for t_ in _os.environ['K_ORDER'].split(',')]
                    pu2 = pb[4][:, 0:128]
                    ph = pb[5][:, 0:128]
                    for step, c in enumerate(order):
                        i0_, i1_ = c, 8 + c
                        P.pe(lambda c=c: nc.tensor.matmul(pw, lhsT=rat[:, c, 128:256], rhs=Hbd[:], start=True, stop=False), reads=['rat_a', 'Hb'], writes=['pb6'])
                        for e in range(2):
                            inst = e * 8 + c
                            P.pe(lambda c=c, e=e, inst=inst: nc.tensor.matmul(pw[:, e * 64:(e + 1) * 64], lhsT=ATall[:, inst, 128:256], rhs=VT[:, c, e * 64:(e + 1) * 64], start=False, stop=(e == 1)), reads=[('AT', inst), 'VT'], writes=['pb6'])
                        P.act(lambda: A.copy(out=Wsb[:], in_=pw), reads=['pb6'], writes=['Wsb'])
                        P.cut(9.1)
                        for e in range(2):
                            inst = e * 8 + c
                            P.pe(lambda e=e, inst=inst: nc.tensor.matmul(pu[:, e * 64:(e + 1) * 64], lhsT=ATall[:, inst, 384:512], rhs=Wsb[:, e * 64:(e + 1) * 64], start=True, stop=True), reads=[('ATq', inst), 'Wsb'], writes=['pb6'])
                        for e in range(2):
                            inst = e * 8 + c
                            P.pe(lambda e=e, inst=inst: nc.tensor.matmul(pu2[:, e * 64:(e + 1) * 64], lhsT=ATall[:, inst, 384:512], rhs=Wsb[:, e * 64:(e + 1) * 64], start=True, stop=True), reads=[('ATq', inst), 'Wsb'], writes=['pb4'])
                        P.dve(lambda: V.tensor_copy(out=Usb[:], in_=pu), reads=['pb6'], writes=['Usb'])
                        P.act(lambda: A.copy(out=Uz_diag, in_=pu2.rearrange("p (e i) -> p e i", e=2)), reads=['pb4'], writes=['Uz'])
                        P.cut(9.2)
                        P.pe(lambda c=c: nc.tensor.matmul(py, lhsT=Hbd[:], rhs=rat[:, c, 0:128], start=True, stop=False), reads=['Hb', 'rat_r'], writes=['pb7'])
                        for e in range(2):
                            inst = e * 8 + c
                            P.pe(lambda c=c, e=e, inst=inst: nc.tensor.matmul(py, lhsT=VTz[:, e, c, :], rhs=ATall[:, inst, 0:128], start=False, stop=False), reads=['VTz', ('AT', inst)], writes=['pb7'])
                        P.pe(lambda c=c: nc.tensor.matmul(ph, lhsT=KT[:, c, :], rhs=VT[:, c, :], start=True, stop=False), reads=['KT', 'VT'], writes=['pb5'])
                        P.pe(lambda c=c: nc.tensor.matmul(ph, lhsT=BT[:, c, :], rhs=Usb[:], start=False, stop=True), reads=['BT', 'Usb'], writes=['pb5'])
                        for e in range(2):
                            inst = e * 8 + c
                            P.pe(lambda c=c, e=e, inst=inst: nc.tensor.matmul(py, lhsT=Uz[:, e, :], rhs=ATall[:, inst, 256:384], start=False, stop=(e == 1)), reads=['Uz', ('AT', inst)], writes=['pb7'])
                        P.cut(9.4)
                        boundary = (step % 2 == 1) and not _os.environ.get('K_NOB')
                        if not boundary and step < 7:
                            for e in range(2):
                                r0 = 64 * e
                                P.dve(lambda c=c, d=d, r0=r0: V.scalar_tensor_tensor(out=Hbd[r0:r0 + 64, r0:r0 + 64], in0=H32d[r0:r0 + 64, r0:r0 + 64], scalar=gC[r0:r0 + 64, d, c:c + 1], in1=ph[r0:r0 + 64, r0:r0 + 64], op0=ALU.mult, op1=ALU.add), reads=['H32', ('gC', d), 'pb5'], writes=['Hb'])
                        for e in range(2):
                            r0 = 64 * e
                            P.dve(lambda c=c, d=d, r0=r0: V.scalar_tensor_tensor(out=H32d[r0:r0 + 64, r0:r0 + 64], in0=H32d[r0:r0 + 64, r0:r0 + 64], scalar=gC[r0:r0 + 64, d, c:c + 1], in1=ph[r0:r0 + 64, r0:r0 + 64], op0=ALU.mult, op1=ALU.add), reads=['H32', ('gC', d), 'pb5'], writes=['H32'])
                        if d == 0:
                            P.act(lambda c=c: A.copy(out=ysb[:, c * 128:(c + 1) * 128], in_=py), reads=['pb7'], writes=['ysb'])
                        else:
                            P.dve(lambda c=c: V.tensor_tensor(out=ysb[:, c * 128:(c + 1) * 128], in0=py, in1=ysb[:, c * 128:(c + 1) * 128], op=ALU.add), reads=['pb7', 'ysb'], writes=['ysb'])
                        P.cut(9.5)
                        if boundary:
                            q = c // 2
                            for e in range(2):
                                r0 = 64 * e
                                P.act(lambda d=d, q=q, r0=r0: A.copy(out=stst[r0:r0 + 64, d, q, :], in_=H32d[r0:r0 + 64, r0:r0 + 64]), reads=['H32'], writes=[('stst', d)])
                            if step < 7:
                                P.dve(lambda: V.tensor_scalar(out=H32d[:], in0=H32d[:], scalar1=C("flag"), scalar2=None, op0=ALU.mult), reads=['H32', 'cols'], writes=['H32'])
                                P.act(lambda: A.copy(out=Hbd[:], in_=H32d[:]), reads=['H32'], writes=['Hb'])
                        if pend:
                            for it in pend.pop(0):
                                it()
                    while pend:
                        for it in pend.pop(0):
                            it()
                    if d == 1 and p + 2 < npairs:
                        wks[p + 2] = pair_w(p + 2)
                    P.cut(10)
                    P.dma('sp', lambda p=p, d=d: nc.sync.dma_start(out=sto_d[d, :, p].rearrange("q r i -> r q i"), in_=stst[:, d, :, :]), reads=[('stst', d)], key=('sto', d))
                    if p == 0 and d == 0:
                        dbg_dump("ysb0", ysb[:], ['ysb'])
                P.cut(11)
                P.capture()
                if p == 0:
                    dbg_dump("ysb", ysb[:], ['ysb'])
                for tb in range(2):
                    P.pe(lambda tb=tb: nc.tensor.matmul(pb[tb][:], lhsT=bones[:], rhs=ysb[:, tb * 512:(tb + 1) * 512], start=True, stop=True), reads=['bones', 'ysb'], writes=['pb%d' % tb])
                    P.dve(lambda tb=tb: V.scalar_tensor_tensor(out=ft1[:, tb * 512:(tb + 1) * 512], in0=pb[tb][:], scalar=-1.0 / 64, in1=ysb[:, tb * 512:(tb + 1) * 512], op0=ALU.mult, op1=ALU.add), reads=['pb%d' % tb, 'ysb'], writes=['ft1'])
                P.act(lambda: A.activation(out=ft2[:], in_=ft1[:], func=AF.Square), reads=['ft1'], writes=['ft2'])
                for tb in range(2):
                    P.pe(lambda tb=tb: nc.tensor.matmul(pb[2 + tb][:], lhsT=bones[:], rhs=ft2[:, tb * 512:(tb + 1) * 512], start=True, stop=True), reads=['bones', 'ft2'], writes=['pb%d' % (2 + tb)])
                    P.dve(lambda tb=tb: V.tensor_scalar(out=fsig[:, tb * 512:(tb + 1) * 512], in0=pb[2 + tb][:], scalar1=1.0 / 64, scalar2=GN_EPS, op0=ALU.mult, op1=ALU.add), reads=['pb%d' % (2 + tb)], writes=['fsig'])
                P.act(lambda: A.activation(out=fsig[:], in_=fsig[:], func=AF.Sqrt), reads=['fsig'], writes=['fsig'])
                P.dve(lambda: V.reciprocal(out=fsig[:], in_=fsig[:]), reads=['fsig'], writes=['fsig'])
                P.dve(lambda: V.tensor_tensor(out=ft1[:], in0=ft1[:], in1=fsig[:], op=ALU.mult), reads=['ft1', 'fsig'], writes=['ft1'])
                P.pool(lambda p=p: G.tensor_scalar(out=ft1[:], in0=ft1[:], scalar1=C("lg", p), scalar2=C("lb", p), op0=ALU.mult, op1=ALU.add), reads=['ft1', 'cols'], writes=['ft1'])
                for tb in range(2):
                    P.pe(lambda tb=tb: nc.tensor.matmul(pb[tb][:], lhsT=bones[:], rhs=fs[:, tb * 512:(tb + 1) * 512], start=True, stop=True), reads=['bones', 'fs'], writes=['pb%d' % tb])
                    P.dve(lambda tb=tb, fvp=fvp: V.tensor_tensor(out=ft2[:, tb * 512:(tb + 1) * 512], in0=pb[tb][:], in1=fvp[:, tb * 512:(tb + 1) * 512], op=ALU.mult), reads=['pb%d' % tb, fvk], writes=['ft2'])
                P.dve(lambda: V.tensor_tensor(out=ft1[:], in0=ft1[:], in1=ft2[:], op=ALU.add), reads=['ft1', 'ft2'], writes=['ft1'])
                P.dve(lambda p=p, sgp=sgp: V.tensor_tensor(out=yA[:, p, :], in0=ft1[:], in1=sgp[:], op=ALU.mult), reads=['ft1', sgk], writes=[('yA', p)])
                if p == 0:
                    dbg_dump("yA0", yA[:, 0, :], [('yA', 0)])
                cap_post = P.end_capture()
                cap_sh = []
                if p + 1 < npairs:
                    P.capture()
                    emit_shared(p + 1)
                    cap_sh = P.end_capture()
                P.replay(cap_post, cap_sh)
            P.muted = False
            _pad = _os.environ.get('K_PAD')
            if _pad:
                kind, npad = _pad[0], int(_pad[1:])
                for i_ in range(npad):
                    if kind == 'A':
                        P.dve(lambda: nc.vector.tensor_copy(out=scr[0:1, 5:6], in_=scr[0:1, 4:5]), reads=[], writes=[])
                    elif kind == 'B':
                        P.dve(lambda: nc.vector.tensor_copy(out=scr[0:1, 5:6], in_=scr[0:1, 4:5]), reads=['padb'], writes=['pada'])
                        P.act(lambda: nc.scalar.copy(out=scr[0:1, 6:7], in_=scr[0:1, 5:6]), reads=['pada'], writes=['padb'])
                    elif kind == 'C':
                        P.dve(lambda: nc.vector.tensor_copy(out=Wsb[:], in_=Wsb[:]), reads=['pb6'], writes=['Wsb'])
                        P.pe(lambda: nc.tensor.matmul(pb[6][:, 0:128], lhsT=Wsb[:], rhs=Wsb[:], start=True, stop=True), reads=['Wsb'], writes=['pb6'])
            dbg_dump("yA", yA[:], [('yA', p) for p in range(16)])
        P.barrier(scr)
        YA_ALL = [('yA', p) for p in range(16)]

        if True:
            s2 = Arena(YA_END)
            yB = sb("yB", [128, KC, T], BF16, s2)
            wsl3 = sb("wsl3", [128, KC, 512], BF16, s2)
            slots = [(wslot, 'wslot'), (wsl2, 'wsl2')]
            slots3 = [(wslot, 'wslot'), (wsl2, 'wsl2'), (wsl3, 'wsl3')]
            items = ([('B', q_) for q_ in range(16 if upto >= 2 else 0)] + [('M', q_) for q_ in range(16 if upto >= 3 else 0)])
            wkeys = {}

            def issue(idx):
                kind, q_ = items[idx]
                slot_, key_ = slots3[idx % 3]
                if kind == 'B':
                    wkeys[idx] = load_w(slot_, key_, [(0, 512, winv[:, :, 8576:8576 + 8192].rearrange("p k (g f) -> p k g f", g=4)[:, :, :, q_ * 128:(q_ + 1) * 128])])
                else:
                    wkeys[idx] = load_w(slot_, key_, [
                        (0, 128, woav[:, :, q_ * 128:(q_ + 1) * 128]),
                        (128, 128, wobv[:, :, q_ * 128:(q_ + 1) * 128]),
                        (256, 256, winv[:, :, 16768:16768 + 4096].rearrange("p k (g f) -> p k g f", g=2)[:, :, :, q_ * 128:(q_ + 1) * 128])])

            def prefetch(idx):
                if idx == 0:
                    for j_ in range(min(2, len(items))):
                        issue(j_)
                if idx + 2 < len(items):
                    issue(idx + 2)
            cx = sb("cx", [128, T], F32, s2)
            cg = sb("cg", [128, T], F32, s2)
            u = sb("u", [128, T], F32, s2)
            sg2 = sb("sg2", [128, T], F32, s2)
            V = nc.vector
            G = nc.gpsimd
            A = nc.scalar
            for q in range(16 if upto >= 2 else 0):
                prefetch(q)
                slot, skey = slots3[q % 3]
                wk = wkeys[q]
                for tb in range(2):
                    ts_ = slice(tb * 512, (tb + 1) * 512)
                    hfn = lambda kc, tb=tb: hT[:, kc, tb * 512:(tb + 1) * 512]
                    b0 = (tb * 4)
                    for fi in range(4):
                        mm_group(pb[b0 + fi], 'pb%d' % (b0 + fi), slot, wk, fi * 128, 128, hfn, HT_ALL)
                    P.act(lambda ts_=ts_, b0=b0: A.copy(out=cg[:, ts_], in_=pb[b0 + 1][:]), reads=['pb%d' % (b0 + 1)], writes=['cg'])
                    P.dve(lambda ts_=ts_, b0=b0: V.tensor_tensor(out=cx[:, ts_], in0=pb[b0 + 2][:], in1=cg[:, ts_], op=ALU.mult), reads=['pb%d' % (b0 + 2), 'cg'], writes=['cx'])
                    P.act(lambda ts_=ts_, b0=b0: A.activation(out=sg2[:, ts_], in_=pb[b0 + 3][:], func=AF.Silu), reads=['pb%d' % (b0 + 3)], writes=['sg2'])
                    P.dve(lambda ts_=ts_, b0=b0: V.tensor_tensor(out=sg2[:, ts_], in0=pb[b0][:], in1=sg2[:, ts_], op=ALU.mult), reads=['pb%d' % b0, 'sg2'], writes=['sg2'])
                cx3 = cx[:].rearrange("p (r t) -> p r t", t=64)
                u3 = u[:].rearrange("p (r t) -> p r t", t=64)
                P.act(lambda q=q: A.activation(out=u[:], in_=cx[:], func=AF.Copy, scale=C("cw", 16 + q)), reads=['cx', 'cols'], writes=['u'])
                P.dve(lambda q=q: V.scalar_tensor_tensor(out=u3[:, :, 1:64], in0=cx3[:, :, 0:63], scalar=C("cw", q), in1=u3[:, :, 1:64], op0=ALU.mult, op1=ALU.add), reads=['cx', 'cols', 'u'], writes=['u'])
                P.dve(lambda q=q: V.scalar_tensor_tensor(out=u3[:, :, 0:63], in0=cx3[:, :, 1:64], scalar=C("cw", 32 + q), in1=u3[:, :, 0:63], op0=ALU.mult, op1=ALU.add), reads=['cx', 'cols', 'u'], writes=['u'])
                cx4 = cx[:].rearrange("p (s r t) -> p s r t", s=4, t=64)
                u4 = u[:].rearrange("p (s r t) -> p s r t", s=4, t=64)
                P.dve(lambda q=q: V.scalar_tensor_tensor(out=u4[:, :, 1:4, 0], in0=cx4[:, :, 0:3, 63], scalar=DC("cw0n", q), in1=u4[:, :, 1:4, 0], op0=ALU.mult, op1=ALU.add), reads=['cx', 'dcols', 'u'], writes=['u'])
                P.dve(lambda q=q: V.scalar_tensor_tensor(out=u4[:, :, 0:3, 63], in0=cx4[:, :, 1:4, 0], scalar=DC("cw2n", q), in1=u4[:, :, 0:3, 63], op0=ALU.mult, op1=ALU.add), reads=['cx', 'dcols', 'u'], writes=['u'])
                P.pool(lambda q=q: G.tensor_tensor(out=yB[:, q, :], in0=u[:], in1=sg2[:], op=ALU.mult), reads=['u', 'sg2'], writes=[('yB', q)])
            dbg_dump("yB", yB[:], [('yB', q) for q in range(16)])
            YB_ALL = [('yB', q) for q in range(16)]

            merged = Rbf
            sa = sb("sa", [128, 2, 512], F32, s2)
            sbm = sb("sbm", [128, 2, 512], F32, s2)
            print("phase2/3a arena end", s2.off, "of", ARENA)
            for q in range(16 if upto >= 3 else 0):
                idx_ = (16 if upto >= 2 else 0) + q
                prefetch(idx_)
                slot, skey = slots3[idx_ % 3]
                wk = wkeys[idx_]
                for tb in range(2):
                    ts_ = slice(tb * 512, (tb + 1) * 512)
                    b0 = tb * 4
                    mm_group(pb[b0], 'pb%d' % b0, slot, wk, 0, 128, lambda kc, ts_=ts_: yA[:, kc, ts_], YA_ALL)
                    mm_group(pb[b0 + 1], 'pb%d' % (b0 + 1), slot, wk, 128, 128, lambda kc, ts_=ts_: yB[:, kc, ts_], YB_ALL)
                    mm_group(pb[b0 + 2], 'pb%d' % (b0 + 2), slot, wk, 256, 128, lambda kc, ts_=ts_: hT[:, kc, ts_], HT_ALL)
                    mm_group(pb[b0 + 3], 'pb%d' % (b0 + 3), slot, wk, 384, 128, lambda kc, ts_=ts_: hT[:, kc, ts_], HT_ALL)
                    P.act(lambda tb=tb, b0=b0: A.activation(out=sa[:, tb, :], in_=pb[b0 + 2][:], func=AF.Sigmoid), reads=['pb%d' % (b0 + 2)], writes=[('sa', tb)])
                    P.act(lambda tb=tb, b0=b0: A.activation(out=sbm[:, tb, :], in_=pb[b0 + 3][:], func=AF.Sigmoid), reads=['pb%d' % (b0 + 3)], writes=[('sbm', tb)])
                    P.dve(lambda tb=tb, b0=b0: V.tensor_tensor(out=sa[:, tb, :], in0=pb[b0][:], in1=sa[:, tb, :], op=ALU.mult), reads=['pb%d' % b0, ('sa', tb)], writes=[('sa', tb)])
                    P.dve(lambda tb=tb, b0=b0: V.tensor_tensor(out=sbm[:, tb, :], in0=pb[b0 + 1][:], in1=sbm[:, tb, :], op=ALU.mult), reads=['pb%d' % (b0 + 1), ('sbm', tb)], writes=[('sbm', tb)])
                    P.pool(lambda tb=tb, q=q, ts_=ts_: G.tensor_tensor(out=merged[:, q, ts_], in0=sa[:, tb, :], in1=sbm[:, tb, :], op=ALU.add), reads=[('sa', tb), ('sbm', tb)], writes=[('merged', q)])
            dbg_dump("merged", merged[:], [('merged', q) for q in range(16)])
            MG_ALL = [('merged', q) for q in range(16)]

            P.barrier(scr)
            s3 = Arena(HT_OFF)
            xnew = sb("xnew", [128, 8, D], F32, s3)
            fgb = sb("fgb", [128, D], F32, s3)
            xin = [sb("xin%d" % i, [128, 512], F32, s3) for i in range(4)]
            osq = sb("osq", [128, D], F32, s3)
            ost = sb("ost", [128, 8, 4], F32, s3)
            gate_bc = sb("gate_bc", [128, D], F32, s3)
            rowsb = sb("rowsb", [128, D], F32, s3)
            screp = sb("screp", [128, KC, 128], F32, s3)
            aslot3 = sb("aslot3", [128, KC, 256], F32, s3)
            print("phase3b arena end", s3.off, "of", ARENA)
            P.dma('sp', lambda: nc.sync.dma_start(out=fgb[:], in_=rows_d[:, 1, :]), writes=['fgb'], key='c_fgb')
            P.dma('sp', lambda: nc.sync.dma_start(out=rowsb[:], in_=rows_d[:, 0, :]), writes=['rowsb'], key='c_rows')
            P.dve(lambda: V.tensor_copy(out=screp[:], in_=sc2[:, :, 0:1].to_broadcast([128, KC, 128])), reads=['sc2a'], writes=['screp'])
            asl2 = aslot3[:].rearrange("p k f -> p (k f)")
            for kc in range(KC if upto >= 4 else 0):
                sk = ('aslot3', kc % 2)
                asl = asl2[:, (kc % 2) * 2048:(kc % 2 + 1) * 2048]
                P.dma('sp', lambda kc=kc, asl=asl: nc.sync.dma_start(out=asl, in_=adaw_d[kc * 128:(kc + 1) * 128, 4096:6144]), writes=[sk], key=sk)
                for blk in range(4):
                    P.pe(lambda kc=kc, blk=blk, asl=asl: nc.tensor.matmul(pb[blk][:], lhsT=screp[:, kc, :], rhs=asl[:, blk * 512:(blk + 1) * 512], start=(kc == 0), stop=(kc == KC - 1)),
                         reads=[sk, 'screp'], writes=['pb%d' % blk])
            for blk in range(4 if upto >= 4 else 0):
                P.dve(lambda blk=blk: V.tensor_tensor(out=gate_bc[:, blk * 512:(blk + 1) * 512], in0=pb[blk][:], in1=rowsb[:, blk * 512:(blk + 1) * 512], op=ALU.add),
                      reads=['pb%d' % blk, 'rowsb'], writes=[('gate_bc', blk)])
            for nb in range(4 if upto >= 4 else 0):
                slot, skey = slots[nb % 2]
                ns_ = slice(nb * 512, (nb + 1) * 512)
                wk = load_w(slot, skey, [(0, 512, wov[:, :, ns_])])
                for tt in range(8):
                    bank = pb[tt]
                    bk = 'pb%d' % tt
                    for kc in range(KC):
                        P.pe(lambda kc=kc, tt=tt, bank=bank, slot=slot: nc.tensor.matmul(bank[:], lhsT=merged[:, kc, tt * 128:(tt + 1) * 128], rhs=slot[:, kc, :], start=(kc == 0), stop=(kc == KC - 1)), reads=wk + MG_ALL, writes=[bk])
                    xi = xin[tt % 4]
                    xk = ('xin', tt % 4)
                    P.dma('sp', lambda tt=tt, ns_=ns_, xi=xi: nc.sync.dma_start(out=xi[:], in_=x_d[tt * 128:(tt + 1) * 128, ns_]), writes=[xk], key=xk)
                    P.dve(lambda tt=tt, ns_=ns_, bank=bank: V.tensor_tensor(out=xnew[:, tt, ns_], in0=bank[:], in1=gate_bc[:, ns_], op=ALU.mult), reads=[bk] + [('gate_bc', i) for i in range(4)], writes=[('xnew', tt, nb)])
                    P.pool(lambda tt=tt, ns_=ns_, xi=xi: G.tensor_tensor(out=xnew[:, tt, ns_], in0=xnew[:, tt, ns_], in1=xi[:], op=ALU.add), reads=[('xnew', tt, nb), xk], writes=[('xnew', tt, nb)])
            for tt in range(8 if upto >= 4 else 0):
                XK = [('xnew', tt, nb) for nb in range(4)]
                P.act(lambda tt=tt: A.activation(out=osq[:], in_=xnew[:, tt, :], func=AF.Square), reads=XK, writes=['osq'])
                P.dve(lambda tt=tt: V.tensor_reduce(out=ost[:, tt, 0:1], in_=osq[:], axis=AX.X, op=ALU.add), reads=['osq'], writes=[('ost', tt)])
                P.dve(lambda tt=tt: V.tensor_scalar(out=ost[:, tt, 1:2], in0=ost[:, tt, 0:1], scalar1=1.0 / D, scalar2=RMS_EPS, op0=ALU.mult, op1=ALU.add), reads=[('ost', tt)], writes=[('ost', tt)])
                P.act(lambda tt=tt: A.activation(out=ost[:, tt, 2:3], in_=ost[:, tt, 1:2], func=AF.Sqrt), reads=[('ost', tt)], writes=[('ost', tt)])
                P.dve(lambda tt=tt: V.reciprocal(out=ost[:, tt, 3:4], in_=ost[:, tt, 2:3]), reads=[('ost', tt)], writes=[('ost', tt)])
                P.dve(lambda tt=tt: V.scalar_tensor_tensor(out=xnew[:, tt, :], in0=xnew[:, tt, :], scalar=ost[:, tt, 3:4], in1=fgb[:], op0=ALU.mult, op1=ALU.mult), reads=XK + [('ost', tt), 'fgb'], writes=[('xo', tt)])
                P.dma('sp', lambda tt=tt: nc.sync.dma_start(out=y_d[tt * 128:(tt + 1) * 128, :], in_=xnew[:, tt, :]), reads=[('xo', tt)], key=('yout', tt % 2))
        P.emit()
        stats = P.stats
    return nc, stats


def _col(v):
    v = np.asarray(v, np.float32).reshape(-1, 128)
    return np.ascontiguousarray(v.T)


def _col96(v):
    v = np.asarray(v, np.float32).reshape(4, 96)
    out = np.zeros((128, 4), np.float32)
    out[:96, :] = v.T
    return out


def _host_constants():
    s = np.arange(128)[:, None]
    t = np.arange(128)[None, :]
    incl_f = (t >= s).astype(np.float32)
    strict_f = (t > s).astype(np.float32)
    incl_b = (t <= s).astype(np.float32)
    strict_b = (t < s).astype(np.float32)
    mask4 = np.stack([np.concatenate([incl_f, strict_f, incl_f, strict_f], 1),
                      np.concatenate([incl_b, strict_b, incl_b, strict_b], 1)], 0)
    maskP = np.stack([np.tile(strict_b, (1, 4)), np.tile(strict_f, (1, 4))], 0)
    cm = np.ones((128, T), np.float32)
    cm[:, 0::128] = 0.0
    ident = np.eye(128, dtype=np.float32)
    bones = np.kron(np.eye(2, dtype=np.float32), np.ones((64, 64), np.float32))
    return dict(mask4=mask4.astype(np.float32), maskP=maskP.astype(np.float32), cm=cm, ident=ident, bones=bones)


_CACHE = {}


def _get_program(dbg=()):
    key = tuple(dbg)
    if key not in _CACHE:
        _CACHE[key] = build_program(dbg)
    return _CACHE[key]


def make_in_maps(inp):
    f = lambda a: np.ascontiguousarray(np.asarray(a, np.float32))
    x_prompt = f(inp["x_prompt"])
    x_sample = f(inp["x_sample"])
    c = f(inp["c"])
    c_ctx = f(inp["c_ctx"])
    mu_prev = f(inp["mu_prev"])[0]
    mu_next = f(inp["mu_next"])[0]
    ada_b = f(inp["ada_b"])[0]
    consts = _host_constants()
    base_cols = [
        _col(mu_prev[:6144]), _col(mu_next[:6144]), _col96(mu_prev[6144:]), _col96(mu_next[6144:]),
        _col(f(inp["w0"])[0].reshape(-1)), _col(f(inp["a0"])[0].reshape(-1)),
        _col(f(inp["k_k"])[0]), _col(f(inp["k_a"])[0]), _col(f(inp["r_k"])[0].reshape(-1)),
        _col(f(inp["lnx_g"])[0]), _col(f(inp["lnx_b"])[0]), _col(f(inp["conv_w"])[0].reshape(-1)),
        _col(f(inp["norm_g"])[0]), _col(ada_b[0:D]), _col(ada_b[D:2 * D]),
    ]
    rows = np.stack([np.broadcast_to(ada_b[2 * D:], (128, D)), np.broadcast_to(f(inp["final_g"]), (128, D))], 1)
    rows = np.ascontiguousarray(rows, np.float32)
    shared = dict(
        ada_w=f(inp["ada_w"])[0], w_in=f(inp["w_in"])[0], w2=f(inp["w2"])[0], a2=f(inp["a2"])[0],
        w_out_a=f(inp["w_out_a"])[0], w_out_b=f(inp["w_out_b"])[0], w_o=f(inp["w_o"])[0],
        rows=rows, **consts)
    sf = f(inp["state_wkv_fwd"])
    sbw = f(inp["state_wkv_bwd"])
    in_maps = []
    for core in range(8):
        if core < 4:
            x = x_sample[core]
            cv = c[core]
            flag = 1.0
            stt = np.stack([sf[core, 0], sbw[core, 0]], 0)
        else:
            x = x_prompt[4 * (core - 4):4 * (core - 3)].reshape(T, D)
            cv = c_ctx
            flag = 0.0
            stt = np.zeros((2, NH, 64, 64), np.float32)
        cols = np.concatenate(base_cols + [np.full((128, 1), flag, np.float32)], 1)
        assert cols.shape[1] == NCOLS
        m = dict(shared)
        m.update(x=np.ascontiguousarray(x), cvec=_col(cv), cols=np.ascontiguousarray(cols), st=np.ascontiguousarray(stt))
        in_maps.append(m)
    return in_maps


def kernel(**inp):
    nc, _ = _get_program()
    in_maps = make_in_maps(inp)
    res = run_bass_kernel_spmd(nc, in_maps, core_ids=list(range(8)))
    r = res.results
    y_sample = np.stack([r[i]["y"] for i in range(4)], 0).astype(np.float32)
    y_prompt = np.concatenate([r[i]["y"].reshape(4, 256, D) for i in range(4, 8)], 0).astype(np.float32)
    def st_out(d):
        a = np.concatenate([r[i]["sto"][d] for i in range(4, 8)], 0)
        a = a.reshape(16, 16, 2, 64, 64).transpose(0, 1, 2, 4, 3)
        return np.ascontiguousarray(a.reshape(16, 1, NH, 64, 64).astype(np.float32))
    nf = st_out(0)
    nb = st_out(1)
    return (y_prompt, y_sample, nf, nb)
```

```python
import numpy as np
import concourse.bass as bass
import concourse.mybir as mybir
from concourse.bass_utils import run_bass_kernel_spmd
from contextlib import ExitStack

F32 = mybir.dt.float32
BF16 = mybir.dt.bfloat16
AF = mybir.ActivationFunctionType
ALU = mybir.AluOpType
AX = mybir.AxisListType

D = 2048
T = 1024
KC = 16
NH = 32
DIN = 20864
NSH = 6528
CE = float(np.exp(-0.5))
RMS_EPS = 1e-6
GN_EPS = 64 * 1e-5

_o = 0
COL = {}
for _n, _w in [("mup", 48), ("mun", 48), ("mupl", 4), ("munl", 4), ("w0", 32), ("a0", 32), ("kk", 16), ("ka", 16),
               ("rk", 16), ("lg", 16), ("lb", 16), ("cw", 48), ("ng", 16), ("abs", 16), ("abc", 16), ("flag", 1)]:
    COL[_n] = _o
    _o += _w
NCOLS = _o
_o = 0
DCOL = {}
for _n, _w in [("c0", 48), ("c0l", 4), ("nmup", 48), ("nmun", 48), ("nmupl", 4), ("nmunl", 4), ("omka", 16),
               ("cw0n", 16), ("cw2n", 16), ("omf", 1), ("shift", 16), ("s1", 16), ("scale", 16)]:
    DCOL[_n] = _o
    _o += _w
NDCOLS = _o


class Prog:
    def __init__(self, nc, stack):
        self.nc = nc
        self.stack = stack
        self.ops = []
        self.lastw = {}
        self.readers = {}
        self.epoch = []
        self.bar_start = 0
        self.muted = False
        import os as _os
        self.cutlevel = float(_os.environ.get('K_CUT1', '99'))

    def sb(self, name, shape, dt, stack=None):
        return (stack or self.stack).enter_context(self.nc.sbuf_tensor(name, list(shape), dt))

    def ps(self, name, shape, dt):
        return self.stack.enter_context(self.nc.psum_tensor(name, list(shape), dt))

    def capture(self):
        self._cap = []

    def end_capture(self):
        c, self._cap = self._cap, None
        return c

    def replay(self, *caps):
        caps = [list(c) for c in caps if c]
        while caps:
            for c in list(caps):
                eng, fn, reads, writes, dk = c.pop(0)
                self._add(eng, fn, reads, writes, dk)
                if not c:
                    caps.remove(c)

    def replay_spread(self, main, side):
        main, side = list(main), list(side)
        n, m = len(main), len(side)
        j = 0
        for i, (eng, fn, reads, writes, dk) in enumerate(main):
            self._add(eng, fn, reads, writes, dk)
            want = ((i + 1) * m) // max(n, 1)
            while j < want:
                e2, f2, r2, w2, d2 = side[j]
                self._add(e2, f2, r2, w2, d2)
                j += 1
        while j < m:
            e2, f2, r2, w2, d2 = side[j]
            self._add(e2, f2, r2, w2, d2)
            j += 1

    def cut(self, n):
        if n >= self.cutlevel:
            self.muted = True

    def _add(self, eng, fn, reads, writes, dma_key=None):
        if self.muted:
            return -1
        if getattr(self, '_cap', None) is not None:
            self._cap.append((eng, fn, tuple(reads), tuple(writes), dma_key))
            return -1
        idx = len(self.ops)
        deps = set(self.epoch)
        for r in reads:
            if r in self.lastw:
                deps.add(self.lastw[r])
        for w in writes:
            if w in self.lastw:
                deps.add(self.lastw[w])
            for x in self.readers.get(w, {}).values():
                deps.add(x)
        self.ops.append(dict(eng=eng, fn=fn, deps=deps, dma_key=dma_key))
        rk = eng if dma_key is None else ('dma', idx)
        for r in reads:
            self.readers.setdefault(r, {})[rk] = idx
        for w in writes:
            self.lastw[w] = idx
            self.readers[w] = {}
        return idx

    def pe(self, fn, reads=(), writes=()):
        return self._add('pe', fn, reads, writes)

    def act(self, fn, reads=(), writes=()):
        return self._add('act', fn, reads, writes)

    def dve(self, fn, reads=(), writes=()):
        return self._add('dve', fn, reads, writes)

    def pool(self, fn, reads=(), writes=()):
        return self._add('pool', fn, reads, writes)

    def dma(self, queue, fn, reads=(), writes=(), key=None):
        assert key is not None
        return self._add(queue, fn, reads, writes, dma_key=key)

    def barrier(self, scr):
        nc = self.nc
        prev = set()
        last = {}
        for i in range(self.bar_start, len(self.ops)):
            op = self.ops[i]
            if op['dma_key'] is not None:
                prev.add(i)
            else:
                last[op['eng']] = i
        prev |= set(last.values())
        prev |= set(self.epoch)
        ids = []
        for e, fn in [('act', lambda: nc.scalar.copy(out=scr[0:1, 0:1], in_=scr[0:1, 4:5])),
                      ('dve', lambda: nc.vector.tensor_copy(out=scr[0:1, 1:2], in_=scr[0:1, 4:5])),
                      ('pool', lambda: nc.gpsimd.tensor_copy(out=scr[0:1, 2:3], in_=scr[0:1, 4:5]))]:
            idx = len(self.ops)
            self.ops.append(dict(eng=e, fn=fn, deps=set(prev), dma_key=None))
            ids.append(idx)
        self.epoch = ids
        self.bar_start = len(self.ops)
        self.lastw = {}
        self.readers = {}

    def emit(self):
        nc = self.nc
        ops = self.ops
        engs = ['pe', 'act', 'dve', 'pool', 'sp']
        need = set()
        for i, op in enumerate(ops):
            for d in op['deps']:
                po = ops[d]
                if po['dma_key'] is None and op['dma_key'] is None and po['eng'] == 'pe' and op['eng'] == 'pe':
                    continue
                need.add(d)
        ROT = 10 ** 9
        DROT = 10 ** 9
        sems = {e: [] for e in engs}
        dsems = {}
        dcnt = {}
        dfinal = []
        cnt = {e: 0 for e in engs}
        sig = {}
        nsem = [0]

        def newsem():
            nsem[0] += 1
            return self.stack.enter_context(nc.semaphore("q_%d" % nsem[0]))
        for i, op in enumerate(ops):
            if op['dma_key'] is not None:
                k = op['dma_key']
                u = dcnt.get(k, 0)
                if u % DROT == 0:
                    dsems[k] = newsem()
                dcnt[k] = u + 1
                v = (u % DROT + 1) * 16
                sig[i] = (dsems[k], v, 16)
                if v == 16:
                    dfinal.append([dsems[k], v])
                else:
                    dfinal[[id(x[0]) for x in dfinal].index(id(dsems[k]))][1] = v
            elif i in need:
                e = op['eng']
                if cnt[e] % ROT == 0:
                    sems[e].append(newsem())
                cnt[e] += 1
                sig[i] = (sems[e][-1], (cnt[e] - 1) % ROT + 1, 1)
        per = {e: [] for e in engs}
        for i, op in enumerate(ops):
            per[op['eng']].append(i)
        handles = {'pe': nc.tensor, 'act': nc.scalar, 'dve': nc.vector, 'pool': nc.gpsimd, 'sp': nc.sync}

        def run(e):
            h = handles[e]
            waited = {}
            for i in per[e]:
                op = ops[i]
                w = {}
                for d in op['deps']:
                    if d not in sig:
                        continue
                    s, v, _ = sig[d]
                    key = id(s)
                    if waited.get(key, 0) >= v:
                        continue
                    if key not in w or w[key][1] < v:
                        w[key] = (s, v)
                for key, (s, v) in w.items():
                    h.wait_ge(s, v)
                    waited[key] = v
                ins = op['fn']()
                if i in sig:
                    s, v, inc = sig[i]
                    ins.then_inc(s, inc)
            if e == 'sp':
                for s_, v_ in dfinal:
                    h.wait_ge(s_, v_)

        with nc.Block() as block:
            @block.sync
            def _(sync):
                run('sp')

            @block.tensor
            def _(t):
                run('pe')

            @block.scalar
            def _(t):
                run('act')

            @block.vector
            def _(t):
                run('dve')

            @block.gpsimd
            def _(t):
                run('pool')
        self.stats = dict(n_ops=len(ops), per={e: len(per[e]) for e in engs}, signals=dict(cnt), nsem=nsem[0])


class _Stop(Exception):
    pass


def build_program(dbg=(), upto=99, npairs=16):
    import os as _os0
    R_ = (lambda ap: ap.bitcast(mybir.dt.float32r)) if _os0.environ.get('K_F32R') else (lambda ap: ap)
    nc = bass.Bass("TRN2", target_bir_lowering=False)

    def din(name, shape):
        return nc.dram_tensor(name, list(shape), F32, kind="ExternalInput").ap()

    def dout(name, shape):
        return nc.dram_tensor(name, list(shape), F32, kind="ExternalOutput").ap()

    x_d = din("x", [T, D])
    cvec_d = din("cvec", [128, KC])
    cols_d = din("cols", [128, NCOLS])
    st_d = din("st", [2, NH, 64, 64])
    ident_d = din("ident", [128, 128])
    bones_d = din("bones", [128, 128])
    mask4_d = din("mask4", [2, 128, 512])
    maskP_d = din("maskP", [2, 128, 512])
    cm_d = din("cm", [128, T])
    rows_d = din("rows", [128, 2, D])
    adaw_d = din("ada_w", [D, 3 * D])
    win_d = din("w_in", [D, DIN])
    w2_d = din("w2", [2, 96, D])
    a2_d = din("a2", [2, 96, D])
    woa_d = din("w_out_a", [D, D])
    wob_d = din("w_out_b", [D, D])
    wo_d = din("w_o", [D, D])
    y_d = dout("y", [T, D])
    sto_d = dout("sto", [2, 4, 16, 128, 64])
    dbg_t = {n: dout("dbg_" + n, shp) for n, shp in dbg}

    winv = win_d.rearrange("(kc p) f -> p kc f", p=128)
    adav = adaw_d.rearrange("(kc p) f -> p kc f", p=128)
    woav = woa_d.rearrange("(kc p) f -> p kc f", p=128)
    wobv = wob_d.rearrange("(kc p) f -> p kc f", p=128)
    wov = wo_d.rearrange("(kc p) f -> p kc f", p=128)

    with ExitStack() as st:
        P = Prog(nc, st)
        ARENA = 212800
        big = st.enter_context(nc.sbuf_tensor("big", [128, ARENA], mybir.dt.uint8))

        class Arena:
            def __init__(self, off):
                self.off = off

            def at(self, shape, dt, off):
                esz = 4 if dt == F32 else 2
                n = int(np.prod(shape[1:])) * esz
                assert off + n <= ARENA, (off, n, ARENA)
                v = big[:, off:off + n].bitcast(dt)
                if len(shape) == 3:
                    v = v.rearrange("p (a b) -> p a b", a=shape[1])
                elif len(shape) == 4:
                    v = v.rearrange("p (a b c) -> p a b c", a=shape[1], b=shape[2])
                return v, n

        def sb(name, shape, dt, ar):
            v, n = ar.at(shape, dt, ar.off)
            ar.off += (n + 63) // 64 * 64
            return v

        AR = Arena(0)
        cols = sb("cols", [128, NCOLS], F32, AR)
        dcols = sb("dcols", [128, NDCOLS], F32, AR)
        scr = sb("scr", [128, 8], F32, AR)
        sc2 = sb("sc2", [128, KC, 2], F32, AR)
        ident_f = sb("ident_f", [128, 128], F32, AR)
        ident_b = sb("ident_b", [128, 128], BF16, AR)
        bones = sb("bones", [128, 128], F32, AR)
        wslot = sb("wslot", [128, KC, 512], BF16, AR)
        WSL2_OFF = AR.off
        wsl2 = sb("wsl2", [128, KC, 512], BF16, AR)
        R_OFF = AR.off
        Rbf = sb("R", [128, KC, T], BF16, AR)
        HT_OFF = AR.off
        hT = sb("hT", [128, KC, T], BF16, AR)
        pb = [P.ps("pb%d" % i, [128, 512], F32) for i in range(8)]

        def C(name, j=0, n=1, np_=128):
            o = COL[name] + j
            return cols[0:np_, o:o + n]

        def DC(name, j=0, n=1, np_=128):
            o = DCOL[name] + j
            return dcols[0:np_, o:o + n]

        def dbg_dump(name, src_ap, reads):
            if name in dbg_t:
                P.dma('pool', lambda: nc.gpsimd.dma_start(out=dbg_t[name], in_=src_ap), reads=reads, key=('dbg', name))

        P.dma('sp', lambda: nc.sync.dma_start(out=cols[:], in_=cols_d), writes=['cols'], key='c_cols')
        P.dma('sp', lambda: nc.sync.dma_start(out=ident_f[:], in_=ident_d), writes=['ident_f'], key='c_idf')
        P.dma('sp', lambda: nc.sync.dma_start(out=bones[:], in_=bones_d), writes=['bones'], key='c_bones')
        P.dma('pool', lambda: nc.gpsimd.dma_start(out=ident_b[:], in_=ident_d), writes=['ident_b'], key='c_idb')
        P.pool(lambda: nc.gpsimd.memset(scr[:], 0.0), writes=['scr'])

        def dcol_ops():
            V = nc.vector
            rw = dict(reads=['cols'], writes=['dcols'])
            P.dve(lambda: V.tensor_tensor(out=DC("c0", 0, 48), in0=C("mup", 0, 48), in1=C("mun", 0, 48), op=ALU.add), **rw)
            P.dve(lambda: V.tensor_scalar(out=DC("c0", 0, 48), in0=DC("c0", 0, 48), scalar1=-1.0, scalar2=1.0, op0=ALU.mult, op1=ALU.add), reads=['dcols'], writes=['dcols'])
            P.dve(lambda: V.tensor_tensor(out=DC("c0l", 0, 4), in0=C("mupl", 0, 4), in1=C("munl", 0, 4), op=ALU.add), **rw)
            P.dve(lambda: V.tensor_scalar(out=DC("c0l", 0, 4), in0=DC("c0l", 0, 4), scalar1=-1.0, scalar2=1.0, op0=ALU.mult, op1=ALU.add), reads=['dcols'], writes=['dcols'])
            P.dve(lambda: V.tensor_scalar(out=DC("omf"), in0=C("flag"), scalar1=-1.0, scalar2=1.0, op0=ALU.mult, op1=ALU.add), **rw)
            for dn, sn, n in [("nmup", "mup", 48), ("nmun", "mun", 48), ("nmupl", "mupl", 4), ("nmunl", "munl", 4)]:
                P.dve(lambda dn=dn, sn=sn, n=n: V.tensor_scalar(out=DC(dn, 0, n), in0=C(sn, 0, n), scalar1=DC("omf"), scalar2=-1.0, op0=ALU.mult, op1=ALU.mult), reads=['cols', 'dcols'], writes=['dcols'])
            P.dve(lambda: V.tensor_scalar(out=DC("omka", 0, 16), in0=C("ka", 0, 16), scalar1=-1.0, scalar2=1.0, op0=ALU.mult, op1=ALU.add), **rw)
            P.dve(lambda: V.tensor_scalar(out=DC("cw0n", 0, 16), in0=C("cw", 0, 16), scalar1=DC("omf"), scalar2=None, op0=ALU.mult), reads=['cols', 'dcols'], writes=['dcols'])
            P.dve(lambda: V.tensor_scalar(out=DC("cw2n", 0, 16), in0=C("cw", 32, 16), scalar1=DC("omf"), scalar2=None, op0=ALU.mult), reads=['cols', 'dcols'], writes=['dcols'])
        dcol_ops()
        import os as _os
        CUT = int(_os.environ.get('K_CUT', '99'))

        if True:
            s0 = Arena(AR.off)
            cv = sb("cv", [128, KC], F32, s0)
            aslot = [Arena(0).at([128, KC, 512], F32, R_OFF)[0], sb("aslotB", [128, KC, 512], F32, s0)]
            xt = [sb("xt%d" % i, [128, D], F32, s0) for i in range(2)]
            xsq = sb("xsq", [128, D], F32, s0)
            xn = [sb("xn%d" % i, [128, D], BF16, s0) for i in range(2)]
            rst = sb("rst", [128, 8, 4], F32, s0)

            P.dma('sp', lambda: nc.sync.dma_start(out=cv[:], in_=cvec_d), writes=['cv'], key='c_cv')
            P.act(lambda: nc.scalar.activation(out=sc2[:, :, 0], in_=cv[:], func=AF.Silu), reads=['cv'], writes=['sc2a'])
            P.act(lambda: nc.scalar.copy(out=sc2[:, :, 1], in_=sc2[:, :, 0]), reads=['sc2a'], writes=['sc2b'])

            macc = sb("macc", [128, 64], F32, s0)
            ring = [aslot[0][:].rearrange("p k f -> p (k f)")[:, 0:4096], aslot[0][:].rearrange("p k f -> p (k f)")[:, 4096:8192],
                    aslot[1][:].rearrange("p k f -> p (k f)")[:, 0:4096], aslot[1][:].rearrange("p k f -> p (k f)")[:, 4096:8192]]
            for kc in range(KC if CUT >= 2 else 0):
                sl = ring[kc % 4]
                sk = ('aslot', kc % 4)
                P.dma('sp', lambda kc=kc, sl=sl: nc.sync.dma_start(out=sl, in_=adaw_d[kc * 128:(kc + 1) * 128, 0:4096]), writes=[sk], key=sk)
                pbk = pb[kc % 2]
                bk = 'pb%d' % (kc % 2)
                for col in range(32):
                    P.pe(lambda sl=sl, kc=kc, col=col, pbk=pbk: nc.tensor.matmul(pbk[:, 2 * col:2 * col + 2], lhsT=sl[:, col * 128:(col + 1) * 128], rhs=sc2[:, kc, :], start=True, stop=True),
                         reads=[sk, 'sc2a', 'sc2b'], writes=[bk])
                if kc == 0:
                    P.dve(lambda pbk=pbk: nc.vector.tensor_copy(out=macc[:], in_=pbk[:, 0:64]), reads=[bk], writes=['macc'])
                else:
                    P.dve(lambda pbk=pbk: nc.vector.tensor_tensor(out=macc[:], in0=pbk[:, 0:64], in1=macc[:], op=ALU.add), reads=[bk, 'macc'], writes=['macc'])
            modv = macc[:].rearrange("p (c two) -> p c two", two=2)
            P.dve(lambda: nc.vector.tensor_tensor(out=DC("shift", 0, 16), in0=modv[:, 0:16, 0], in1=C("abs", 0, 16), op=ALU.add), reads=['macc', 'cols'], writes=['dc_shift'])
            P.dve(lambda: nc.vector.tensor_tensor(out=DC("scale", 0, 16), in0=modv[:, 16:32, 0], in1=C("abc", 0, 16), op=ALU.add), reads=['macc', 'cols'], writes=['dc_scale'])
            P.dve(lambda: nc.vector.scalar_tensor_tensor(out=DC("s1", 0, 16), in0=DC("scale", 0, 16), scalar=1.0, in1=C("ng", 0, 16), op0=ALU.add, op1=ALU.mult), reads=['dc_scale', 'cols'], writes=['dc_s1'])

            for tt in range(8 if CUT >= 3 else 0):
                xb_ = xt[tt % 2]
                xk = ('xt', tt % 2)
                xnb = xn[tt % 2]
                nk = ('xn', tt % 2)
                P.dma('sp', lambda tt=tt, xb_=xb_: nc.sync.dma_start(out=xb_[:], in_=x_d[tt * 128:(tt + 1) * 128, :]), writes=[xk], key=xk)
                P.act(lambda xb_=xb_: nc.scalar.activation(out=xsq[:], in_=xb_[:], func=AF.Square), reads=[xk], writes=['xsq'])
                P.dve(lambda tt=tt: nc.vector.tensor_reduce(out=rst[:, tt, 0:1], in_=xsq[:], axis=AX.X, op=ALU.add), reads=['xsq'], writes=[('rst', tt)])
                P.dve(lambda tt=tt: nc.vector.tensor_scalar(out=rst[:, tt, 1:2], in0=rst[:, tt, 0:1], scalar1=1.0 / D, scalar2=RMS_EPS, op0=ALU.mult, op1=ALU.add), reads=[('rst', tt)], writes=[('rst', tt)])
                P.act(lambda tt=tt: nc.scalar.activation(out=rst[:, tt, 2:3], in_=rst[:, tt, 1:2], func=AF.Sqrt), reads=[('rst', tt)], writes=[('rst', tt)])
                P.dve(lambda tt=tt: nc.vector.reciprocal(out=rst[:, tt, 3:4], in_=rst[:, tt, 2:3]), reads=[('rst', tt)], writes=[('rst', tt)])
                P.act(lambda tt=tt, xb_=xb_, xnb=xnb: nc.scalar.activation(out=xnb[:], in_=xb_[:], func=AF.Copy, scale=rst[:, tt, 3:4]), reads=[xk, ('rst', tt)], writes=[nk])
                for half in range(2 if CUT >= 4 else 0):
                    bank = pb[2 + half]
                    bk = 'pb%d' % (2 + half)
                    bv = bank[:].bitcast(BF16)
                    for j in range(8):
                        kc = half * 8 + j
                        P.pe(lambda bv=bv, j=j, kc=kc, xnb=xnb: nc.tensor.transpose(bv[:, j * 128:(j + 1) * 128], xnb[:, kc * 128:(kc + 1) * 128], ident_b[:]),
                             reads=[nk, 'ident_b'], writes=[bk])
                    for j in range(8 if CUT >= 5 else 0):
                        kc = half * 8 + j
                        if True:
                            P.dve(lambda bv=bv, j=j, kc=kc, tt=tt: nc.vector.tensor_scalar(out=hT[:, kc, tt * 128:(tt + 1) * 128], in0=bv[:, j * 128:(j + 1) * 128], scalar1=DC("s1", kc), scalar2=DC("shift", kc), op0=ALU.mult, op1=ALU.add),
                                  reads=[bk, 'dc_s1', 'dc_shift'], writes=[('hT', kc)])
                        else:
                            P.act(lambda bv=bv, j=j, kc=kc, tt=tt: nc.scalar.activation(out=hT[:, kc, tt * 128:(tt + 1) * 128], in_=bv[:, j * 128:(j + 1) * 128], func=AF.Identity, scale=DC("s1", kc), bias=DC("shift", kc)),
                                  reads=[bk, 'dc_s1', 'dc_shift'], writes=[('hT', kc)])
            dbg_dump("hT", hT[:], [('hT', kc) for kc in range(KC)])
        P.barrier(scr)
        HT_ALL = [('hT', kc) for kc in range(KC)]
        UPTO = upto

        def load_w(slot, key, pieces):
            flat = []
            for (c0, n, src) in pieces:
                if len(src.shape) == 4:
                    gg = src.shape[2]
                    ff = src.shape[3]
                    for gi in range(gg):
                        flat.append((c0 + gi * ff, ff, src[:, :, gi, :]))
                else:
                    flat.append((c0, n, src))
            for i, (c0, n, src) in enumerate(flat):
                dst = slot[:, :, c0:c0 + n]
                P.dma('pool', lambda dst=dst, src=src: nc.gpsimd.dma_start(out=dst, in_=src), writes=[(key, i)], key=(key, i))
            return [(key, i) for i in range(len(flat))]

        def mm_group(bank, bkey, slot, wkeys, wc0, M, rhs_fn, rkeys, np_out=128):
            for kc in range(KC):
                P.pe(lambda kc=kc: nc.tensor.matmul(bank[0:np_out, :], lhsT=slot[:, kc, wc0:wc0 + M], rhs=rhs_fn(kc), start=(kc == 0), stop=(kc == KC - 1)),
                     reads=list(wkeys) + list(rkeys), writes=[bkey])

        def tshift_items(z, zk, out, ok, np_, c0, mup, mun, nmup, nmun):
            V = nc.vector
            return [
                lambda: P.dve(lambda: V.tensor_scalar(out=out[0:np_, :], in0=z[0:np_, :], scalar1=c0, scalar2=None, op0=ALU.mult), reads=[zk, 'dcols'], writes=[ok]),
                lambda: P.dve(lambda: V.scalar_tensor_tensor(out=out[0:np_, 1:T], in0=z[0:np_, 0:T - 1], scalar=mup, in1=out[0:np_, 1:T], op0=ALU.mult, op1=ALU.add), reads=[zk, 'cols'], writes=[ok]),
                lambda: P.dve(lambda: V.scalar_tensor_tensor(out=out[0:np_, 0:T - 1], in0=z[0:np_, 1:T], scalar=mun, in1=out[0:np_, 0:T - 1], op0=ALU.mult, op1=ALU.add), reads=[zk, 'cols'], writes=[ok]),
                lambda: P.dve(lambda: V.scalar_tensor_tensor(out=out[0:np_, 256:T:256], in0=z[0:np_, 255:T - 1:256], scalar=nmup, in1=out[0:np_, 256:T:256], op0=ALU.mult, op1=ALU.add), reads=[zk, 'dcols'], writes=[ok]),
                lambda: P.dve(lambda: V.scalar_tensor_tensor(out=out[0:np_, 255:T - 1:256], in0=z[0:np_, 256:T:256], scalar=nmun, in1=out[0:np_, 255:T - 1:256], op0=ALU.mult, op1=ALU.add), reads=[zk, 'dcols'], writes=[ok]),
            ]

        def tshift(*a):
            for it in tshift_items(*a):
                it()

        yA = sb("yA", [128, KC, T], BF16, AR)
        YA_END = AR.off
        if True:
            s1 = Arena(YA_END)
            rA = Arena(R_OFF)
            wA = Arena(WSL2_OFF)
            ATall = sb("ATall", [128, 16, 512], BF16, rA)
            rat = sb("rat", [128, 8, 256], BF16, rA)
            ktl = sb("ktl", [128, T], BF16, rA)
            btl = sb("btl", [128, T], BF16, rA)
            kpf = sb("kpf", [128, T], BF16, rA)
            bpf = sb("bpf", [128, T], BF16, rA)
            vbf = sb("vbf", [128, T], BF16, rA)
            sgbs = [sb("sgb0", [128, T], BF16, rA), sb("sgb1", [128, T], BF16, s1)]
            assert rA.off <= R_OFF + 32768
            fsig = sb("f_sig", [128, T], F32, wA)
            fa = sb("f_a", [128, T], F32, wA)
            ft1 = sb("f_t1", [128, T], F32, wA)
            ft2 = sb("f_t2", [128, T], F32, wA)
            assert wA.off <= WSL2_OFF + 16384
            lora = sb("lora", [128, 4, T], BF16, s1)
            lw = sb("lw", [128, 4, 128], BF16, s1)
            mask4 = sb("mask4", [128, 2, 512], BF16, s1)
            maskP = sb("maskP", [128, 2, 512], BF16, s1)
            cm = sb("cm", [128, T], F32, s1)
            fr = sb("f_r", [128, T], F32, s1)
            fk = sb("f_k", [128, T], F32, s1)
            fvs = [sb("f_v%d" % i, [128, T], F32, s1) for i in range(2)]
            fs = sb("f_s", [128, T], F32, s1)
            fL = sb("f_L", [128, T], F32, s1)
            fkk = sb("f_kk", [128, T], F32, s1)
            ysb = sb("ysb", [128, T], F32, s1)
            VT = sb("VT", [128, 8, 128], BF16, s1)
            KT = sb("KT", [128, 8, 128], BF16, s1)
            BT = sb("BT", [128, 8, 128], BF16, s1)
            Pq = [sb("Pq%d" % i, [128, 4, 128], F32, s1) for i in range(2)]
            Qq = [sb("Qq%d" % i, [128, 4, 128], F32, s1) for i in range(2)]
            Nn = sb("Nn", [128, 4, 128], F32, s1)
            H32d = sb("H32d", [128, 128], F32, s1)
            Hbd = sb("Hbd", [128, 128], BF16, s1)
            VTz = sb("VTz", [128, 2, 8, 128], BF16, s1)
            Uz = sb("Uz", [128, 2, 128], BF16, s1)
            Wsb = sb("Wsb", [128, 128], BF16, s1)
            Usb = sb("Usb", [128, 128], BF16, s1)
            stin = sb("stin", [128, 128], F32, s1)
            stst = sb("stst", [128, 2, 4, 64], F32, s1)
            gC = sb("gC", [128, 2, 8], F32, s1)
            LsC = sb("LsC", [128, 8], F32, s1)
            nLsC = sb("nLsC", [128, 8], F32, s1)
            print("phase1 arena end", s1.off, "of", ARENA)

            P.dma('pool', lambda: nc.gpsimd.dma_start(out=mask4[:], in_=mask4_d.rearrange("d p f -> p d f")), writes=['mask4'], key='c_m4')
            P.dma('pool', lambda: nc.gpsimd.dma_start(out=maskP[:], in_=maskP_d.rearrange("d p f -> p d f")), writes=['maskP'], key='c_mP')
            P.dma('sp', lambda: nc.sync.dma_start(out=cm[:], in_=cm_d), writes=['cm'], key='c_cm')
            P.pool(lambda: nc.gpsimd.memset(stin[:], 0.0), writes=['stin0'])
            P.pool(lambda: nc.gpsimd.memset(VTz[:].rearrange("p e c f -> p (e c f)"), 0.0), writes=['VTz'])
            P.pool(lambda: nc.gpsimd.memset(Uz[:].rearrange("p e f -> p (e f)"), 0.0), writes=['Uz'])
            _uzf = Uz[:].rearrange("p e f -> p (e f)")
            Uz_diag = bass.AP(tensor=_uzf.tensor, offset=_uzf.offset, ap=[list(_uzf.ap[0]), [192, 2], [1, 64]])

            wk = load_w(wslot, 'wslot', [(0, 384, winv[:, :, 6144:6528])])
            for i in range(4 if upto >= 1 else 0):
                for tb in range(2):
                    bank = pb[tb]
                    mm_group(bank, 'pb%d' % tb, wslot, wk, 96 * i, 96, lambda kc, tb=tb: hT[:, kc, tb * 512:(tb + 1) * 512], HT_ALL, np_out=96)
                    P.act(lambda tb=tb, bank=bank: nc.scalar.copy(out=fL[0:96, tb * 512:(tb + 1) * 512], in_=bank[0:96, :]), reads=['pb%d' % tb], writes=['fL'])
                tshift(fL, 'fL', ft1, 'ft1', 96, DC("c0l", i, 1, 96), C("mupl", i, 1, 96), C("munl", i, 1, 96), DC("nmupl", i, 1, 96), DC("nmunl", i, 1, 96))
                P.act(lambda i=i: nc.scalar.activation(out=lora[0:96, i, :], in_=ft1[0:96, :], func=(AF.Tanh if i < 2 else AF.Copy)), reads=['ft1'], writes=[('lora', i)])

            wks = {}

            def pair_w(pp):
                return load_w(wslot, 'wslot', [
                    (0, 384, winv[:, :, 0:6144].rearrange("p k (g f) -> p k g f", g=3)[:, :, :, pp * 128:(pp + 1) * 128]),
                    (384, 128, winv[:, :, NSH + pp * 128:NSH + (pp + 1) * 128])])

            def main_items(pp):
                A_ = nc.scalar
                wk_ = wks[pp]
                dsts = [(fr, 'fr'), (fk, 'fk'), (fvs[pp % 2], 'fv%d' % (pp % 2)), (None, None)]

                def mm(fi, tb):
                    bank = pb[(fi % 2) * 2 + tb]
                    bk = 'pb%d' % ((fi % 2) * 2 + tb)
                    return lambda: mm_group(bank, bk, wslot, wk_, fi * 128, 128, lambda kc: hT[:, kc, tb * 512:(tb + 1) * 512], HT_ALL)

                def ev(fi, tb):
                    bank = pb[(fi % 2) * 2 + tb]
                    bk = 'pb%d' % ((fi % 2) * 2 + tb)
                    if fi < 3:
                        return lambda: P.act(lambda: A_.copy(out=fL[:, tb * 512:(tb + 1) * 512], in_=bank[:]), reads=[bk], writes=['fL'])
                    sg_, sk_ = sgbs[pp % 2], 'sgb%d' % (pp % 2)
                    return lambda: P.act(lambda: A_.activation(out=sg_[:, tb * 512:(tb + 1) * 512], in_=bank[:], func=AF.Silu), reads=[bk], writes=[sk_])

                def ts(fi):
                    ti = fi * 16 + pp
                    return tshift_items(fL, 'fL', dsts[fi][0], dsts[fi][1], 128, DC("c0", ti), C("mup", ti), C("mun", ti), DC("nmup", ti), DC("nmun", ti))
                t0_, t1_, t2_ = ts(0), ts(1), ts(2)
                items = [[mm(0, 0)],
                         [mm(0, 1), ev(0, 0)],
                         [mm(1, 0), ev(0, 1), t0_[0], t0_[1]],
                         [mm(1, 1), t0_[2], t0_[3], t0_[4]],
                         [mm(2, 0), ev(1, 0), ev(1, 1), t1_[0]],
                         [mm(2, 1), t1_[1], t1_[2], t1_[3], t1_[4]],
                         [mm(3, 0), ev(2, 0), ev(2, 1), t2_[0], t2_[1]],
                         [mm(3, 1), t2_[2], t2_[3], t2_[4], ev(3, 0)],
                         [ev(3, 1)]]
                return items

            def emit_shared(pp):
                V = nc.vector
                A = nc.scalar
                fvp, fvk = fvs[pp % 2], 'fv%d' % (pp % 2)
                P.act(lambda fvp=fvp: A.copy(out=vbf[:], in_=fvp[:]), reads=[fvk], writes=['vbf'])
                P.act(lambda p=pp: A.activation(out=fkk[:], in_=fk[:], func=AF.Copy, scale=C("kk", p)), reads=['fk', 'cols'], writes=['fkk'])
                P.act(lambda: A.activation(out=fL[:], in_=fkk[:], func=AF.Square), reads=['fkk'], writes=['fL'])
                for tb in range(2):
                    P.pe(lambda tb=tb: nc.tensor.matmul(pb[6 + tb][:], lhsT=bones[:], rhs=fL[:, tb * 512:(tb + 1) * 512], start=True, stop=True), reads=['bones', 'fL'], writes=['pb%d' % (6 + tb)])
                    P.act(lambda tb=tb: A.activation(out=fa[:, tb * 512:(tb + 1) * 512], in_=pb[6 + tb][:], func=AF.Sqrt), reads=['pb%d' % (6 + tb)], writes=['fa'])
                P.dve(lambda: V.tensor_scalar(out=fa[:], in0=fa[:], scalar1=1e-12, scalar2=None, op0=ALU.max), reads=['fa'], writes=['fa'])
                P.dve(lambda: V.reciprocal(out=fa[:], in_=fa[:]), reads=['fa'], writes=['fa'])
                P.dve(lambda: V.tensor_tensor(out=fkk[:], in0=fkk[:], in1=fa[:], op=ALU.mult), reads=['fkk', 'fa'], writes=['fkk'])
                vbank = pb[4][:].bitcast(BF16)
                for c in range(8):
                    P.pe(lambda c=c: nc.tensor.transpose(vbank[:, c * 128:(c + 1) * 128], vbf[:, c * 128:(c + 1) * 128], ident_b[:]), reads=['vbf', 'ident_b'], writes=['pb4'])
                P.act(lambda: A.copy(out=VT[:].rearrange("p c f -> p (c f)"), in_=vbank[:]), reads=['pb4'], writes=['VT', 'lk4'])
                vb3 = vbank[:].rearrange("p (c f) -> p c f", f=128)
                P.dve(lambda vb3=vb3: V.tensor_copy(out=VTz[:, 0, :, 0:64], in_=vb3[:, :, 0:64]), reads=['pb4'], writes=['VTz', 'lk4'])
                P.act(lambda vb3=vb3: A.copy(out=VTz[:, 1, :, 64:128], in_=vb3[:, :, 64:128]), reads=['pb4'], writes=['VTz', 'lk4'])
                if pp == 0:
                    dbg_dump("fr", fr[:], ['fr'])
                    dbg_dump("fk", fk[:], ['fk'])
                    dbg_dump("fv", fvp[:], [fvk])
                    dbg_dump("fkk", fkk[:], ['fkk'])


            for p in range(npairs if upto >= 1 else 0):
                V = nc.vector
                G = nc.gpsimd
                A = nc.scalar
                fvp, fvk = fvs[p % 2], 'fv%d' % (p % 2)
                sgp, sgk = sgbs[p % 2], 'sgb%d' % (p % 2)
                if p == 0:
                    wks[0] = pair_w(0)
                    for grp in main_items(0):
                        for it in grp:
                            it()
                    if npairs > 1:
                        wks[1] = pair_w(1)
                P.dma('pool', lambda p=p: nc.gpsimd.dma_start(out=lw[0:96, 0:2, :], in_=w2_d[:, :, p * 128:(p + 1) * 128].rearrange("d r f -> r d f")), writes=['lw0'], key='lw0')
                P.dma('pool', lambda p=p: nc.gpsimd.dma_start(out=lw[0:96, 2:4, :], in_=a2_d[:, :, p * 128:(p + 1) * 128].rearrange("d r f -> r d f")), writes=['lw1'], key='lw1')
                P.cut(1)
                if p == 0:
                    emit_shared(0)
                def prep_front(d, wb=(0, 1), ab=(2, 3)):
                    for tb in range(2):
                        P.pe(lambda tb=tb, d=d: nc.tensor.matmul(pb[wb[tb]][:], lhsT=lw[0:96, d, :], rhs=lora[0:96, d, tb * 512:(tb + 1) * 512], start=True, stop=True), reads=['lw0', ('lora', d)], writes=['pb%d' % wb[tb]])
                        P.act(lambda tb=tb, d=d, p=p: A.activation(out=fsig[:, tb * 512:(tb + 1) * 512], in_=pb[wb[tb]][:], func=AF.Sigmoid, bias=C("w0", d * 16 + p)), reads=['pb%d' % wb[tb], 'cols'], writes=['fsig'])
                        P.pe(lambda tb=tb, d=d: nc.tensor.matmul(pb[ab[tb]][:], lhsT=lw[0:96, 2 + d, :], rhs=lora[0:96, 2 + d, tb * 512:(tb + 1) * 512], start=True, stop=True), reads=['lw1', ('lora', 2 + d)], writes=['pb%d' % ab[tb]])
                        P.act(lambda tb=tb, d=d, p=p: A.activation(out=fa[:, tb * 512:(tb + 1) * 512], in_=pb[ab[tb]][:], func=AF.Sigmoid, bias=C("a0", d * 16 + p)), reads=['pb%d' % ab[tb], 'cols'], writes=['fa'])
                    P.cut(3)
                    if d == 0:
                        P.dve(lambda: V.tensor_tensor_scan(out=fL[:], data0=cm[:], data1=fsig[:], initial=0.0, op0=ALU.mult, op1=ALU.add), reads=['cm', 'fsig'], writes=['fL'])
                        endc = 127
                    else:
                        P.dve(lambda: V.tensor_tensor_scan(out=fL[:, ::-1], data0=cm[:], data1=fsig[:, ::-1], initial=0.0, op0=ALU.mult, op1=ALU.add), reads=['cm', 'fsig'], writes=['fL'])
                        endc = 0
                    fL3 = fL[:].rearrange("p (c t) -> p c t", t=128)
                    P.cut(4)
                    P.pool(lambda endc=endc: G.tensor_copy(out=LsC[:], in_=fL3[:, :, endc]), reads=['fL'], writes=['LsC'])
                    P.dve(lambda: V.tensor_scalar(out=nLsC[:], in0=LsC[:], scalar1=-CE, scalar2=None, op0=ALU.mult), reads=['LsC'], writes=['nLsC'])
                    P.act(lambda d=d: A.activation(out=gC[:, d, :], in_=LsC[:], func=AF.Exp, scale=-CE), reads=['LsC'], writes=[('gC', d)])
                    P.pool(lambda p=p: G.tensor_scalar(out=ft1[:], in0=fa[:], scalar1=C("ka", p), scalar2=DC("omka", p), op0=ALU.mult, op1=ALU.add), reads=['fa', 'cols', 'dcols'], writes=['ft1'])
                    P.pool(lambda: G.tensor_tensor(out=ft2[:], in0=fL[:], in1=fsig[:], op=ALU.subtract), reads=['fL', 'fsig'], writes=['ft2'])
                    P.dve(lambda: V.tensor_tensor(out=ft1[:], in0=ft1[:], in1=fk[:], op=ALU.mult), reads=['ft1', 'fk'], writes=['ft1'])
                    P.dve(lambda: V.tensor_tensor(out=fa[:], in0=fa[:], in1=fkk[:], op=ALU.mult), reads=['fa', 'fkk'], writes=['fa'])
                    if d == 0:
                        P.dve(lambda p=p: V.scalar_tensor_tensor(out=fs[:], in0=fr[:], scalar=C("rk", p), in1=ft1[:], op0=ALU.mult, op1=ALU.mult), reads=['fr', 'ft1', 'cols'], writes=['fs'])
                    else:
                        P.dve(lambda p=p: V.scalar_tensor_tensor(out=fsig[:], in0=fr[:], scalar=C("rk", p), in1=ft1[:], op0=ALU.mult, op1=ALU.mult), reads=['fr', 'ft1', 'cols', 'ft2'], writes=['fsig'])

                for d in range(2):
                    if d == 0:
                        prep_front(0)
                    fL3 = fL[:].rearrange("p (c t) -> p c t", t=128)
                    rat3 = rat[:]
                    fr3 = fr[:].rearrange("p (c t) -> p c t", t=128)
                    fkk3 = fkk[:].rearrange("p (c t) -> p c t", t=128)
                    P.act(lambda: A.activation(out=rat3[:, :, 0:128], in_=fL3, func=AF.Exp, scale=-CE), reads=['fL'], writes=['rat_r'])
                    P.act(lambda: A.activation(out=btl[:], in_=fL[:], func=AF.Exp, scale=CE), reads=['fL'], writes=['btl'])
                    P.act(lambda: A.activation(out=rat3[:, :, 128:256], in_=ft2[:].rearrange("p (c t) -> p c t", t=128), func=AF.Exp, scale=-CE), reads=['ft2'], writes=['rat_a'])
                    for c in range(8):
                        P.act(lambda c=c: A.activation(out=bpf[:, c * 128:(c + 1) * 128], in_=fL[:, c * 128:(c + 1) * 128], func=AF.Exp, scale=CE, bias=nLsC[:, c:c + 1]), reads=['fL', 'nLsC'], writes=['bpf'])
                    P.dve(lambda: V.tensor_tensor(out=rat3[:, :, 0:128], in0=rat3[:, :, 0:128], in1=fr3, op=ALU.mult), reads=['fr', 'rat_r'], writes=['rat_r'])
                    P.dve(lambda: V.tensor_tensor(out=ktl[:], in0=btl[:], in1=ft1[:], op=ALU.mult), reads=['ft1', 'btl'], writes=['ktl'])
                    P.dve(lambda: V.tensor_tensor(out=btl[:], in0=btl[:], in1=fa[:], op=ALU.mult), reads=['fa', 'btl'], writes=['btl'])
                    P.dve(lambda: V.scalar_tensor_tensor(out=rat3[:, :, 128:256], in0=rat3[:, :, 128:256], scalar=-1.0, in1=fkk3, op0=ALU.mult, op1=ALU.mult), reads=['fkk', 'rat_a'], writes=['rat_a'])
                    P.dve(lambda: V.tensor_tensor(out=kpf[:], in0=bpf[:], in1=ft1[:], op=ALU.mult), reads=['ft1', 'bpf'], writes=['kpf'])
                    P.pool(lambda: G.tensor_tensor(out=bpf[:], in0=bpf[:], in1=fa[:], op=ALU.mult), reads=['fa', 'bpf'], writes=['bpf'])
                    if d == 1:
                        P.pool(lambda: G.tensor_tensor(out=fs[:], in0=fs[:], in1=fsig[:], op=ALU.add), reads=['fs', 'fsig'], writes=['fs'])
                    P.cut(5)
                    if p == 0 and d == 0:
                        dbg_dump("rat", rat[:], ['rat_r', 'rat_a'])
                        dbg_dump("ktl", ktl[:], ['ktl'])
                        dbg_dump("KT", KT[:], ['KT'])
                    P.cut(6)
                    RAT = ['rat_r', 'rat_a']
                    P.capture()
                    for c in range(8):
                        for e in range(2):
                            inst = e * 8 + c
                            bank = pb[3 + inst % 3]
                            bk = 'pb%d' % (3 + inst % 3)
                            r0 = 64 * e
                            P.pe(lambda c=c, r0=r0, bank=bank: nc.tensor.matmul(bank[:, 0:256], lhsT=ktl[r0:r0 + 64, c * 128:(c + 1) * 128], rhs=rat[r0:r0 + 64, c, :], start=True, stop=True), reads=['ktl'] + RAT, writes=[bk])
                            P.pe(lambda c=c, r0=r0, bank=bank: nc.tensor.matmul(bank[:, 256:384], lhsT=btl[r0:r0 + 64, c * 128:(c + 1) * 128], rhs=rat[r0:r0 + 64, c, 0:128], start=True, stop=True), reads=['btl', 'rat_r'], writes=[bk])
                            P.act(lambda inst=inst, bank=bank: A.copy(out=ATall[:, inst, 0:384], in_=bank[:, 0:384]), reads=[bk], writes=[('AT', inst)])
                            P.dve(lambda inst=inst, d=d: V.tensor_tensor(out=ATall[:, inst, 0:384], in0=ATall[:, inst, 0:384], in1=mask4[:, d, 0:384], op=ALU.mult), reads=[('AT', inst), 'mask4'], writes=[('AT', inst)])
                    for (srcf, sk_, dstT, dk_, bi) in [(kpf, 'kpf', KT, 'KT', 3), (bpf, 'bpf', BT, 'BT', 4)]:
                        tbank = pb[bi][:].bitcast(BF16)
                        for c in range(8):
                            P.pe(lambda c=c, srcf=srcf, tbank=tbank: nc.tensor.transpose(tbank[:, c * 128:(c + 1) * 128], srcf[:, c * 128:(c + 1) * 128], ident_b[:]), reads=[sk_, 'ident_b'], writes=['pb%d' % bi])
                        P.act(lambda dstT=dstT, tbank=tbank: A.copy(out=dstT[:].rearrange("p c f -> p (c f)"), in_=tbank[:]), reads=['pb%d' % bi], writes=[dk_])
                    cap_a = P.end_capture()
                    P.capture()
                    P.cut(7)
                    for bt_ in range(4):
                        e_ = bt_ // 2
                        r0 = 64 * e_
                        cs = [(bt_ % 2) * 4 + i for i in range(4)]
                        insts = [e_ * 8 + c for c in cs]
                        for i, c in enumerate(cs):
                            P.pe(lambda c=c, i=i, r0=r0: nc.tensor.matmul(pb[6][:, i * 128:(i + 1) * 128], lhsT=rat[r0:r0 + 64, c, 128:256], rhs=btl[r0:r0 + 64, c * 128:(c + 1) * 128], start=True, stop=True),
                                 reads=['rat_a', 'btl'], writes=['pb6'])
                        for i, c in enumerate(cs):
                            P.pe(lambda c=c, i=i, r0=r0: nc.tensor.matmul(pb[7][:, i * 128:(i + 1) * 128], lhsT=btl[r0:r0 + 64, c * 128:(c + 1) * 128], rhs=rat[r0:r0 + 64, c, 128:256], start=True, stop=True),
                                 reads=['rat_a', 'btl'], writes=['pb7'])
                        P.dve(lambda d=d: V.tensor_tensor(out=Pq[0][:].rearrange("p i f -> p (i f)"), in0=pb[6][:], in1=maskP[:, d, :], op=ALU.mult), reads=['pb6', 'maskP'], writes=[('Pq', 0)])
                        P.dve(lambda d=d: V.tensor_tensor(out=Qq[0][:].rearrange("p i f -> p (i f)"), in0=pb[7][:], in1=maskP[:, 1 - d, :], op=ALU.mult), reads=['pb7', 'maskP'], writes=[('Qq', 0)])
                        P.pool(lambda: G.tensor_tensor(out=Nn[:], in0=Qq[0][:], in1=ident_f[:].unsqueeze(1).to_broadcast([128, 4, 128]), op=ALU.add), reads=[('Qq', 0), 'ident_f'], writes=['Nn'])
                        for lev in range(1, 7):
                            cur = (lev - 1) % 2
                            nxt = lev % 2
                            bX, bY, bZ = pb[0], pb[1], pb[2]
                            kX, kY, kZ = 'pb0', 'pb1', 'pb2'
                            for i in range(4):
                                P.pe(lambda bX=bX, i=i, cur=cur: nc.tensor.matmul(bX[:, i * 128:(i + 1) * 128], lhsT=R_(Qq[cur][:, i, :]), rhs=R_(Pq[cur][:, i, :]), start=True, stop=True), reads=[('Qq', cur), ('Pq', cur)], writes=[kX])
                            P.act(lambda bX=bX, nxt=nxt: A.copy(out=Pq[nxt][:].rearrange("p i f -> p (i f)"), in_=bX[:]), reads=[kX], writes=[('Pq', nxt)])
                            if lev < 6:
                                for i in range(4):
                                    P.pe(lambda bY=bY, i=i, cur=cur: nc.tensor.matmul(bY[:, i * 128:(i + 1) * 128], lhsT=R_(Pq[cur][:, i, :]), rhs=R_(Qq[cur][:, i, :]), start=True, stop=True), reads=[('Qq', cur), ('Pq', cur)], writes=[kY])
                                P.act(lambda bY=bY, nxt=nxt: A.copy(out=Qq[nxt][:].rearrange("p i f -> p (i f)"), in_=bY[:]), reads=[kY], writes=[('Qq', nxt)])
                            for i in range(4):
                                P.pe(lambda bZ=bZ, i=i, nxt=nxt: nc.tensor.matmul(bZ[:, i * 128:(i + 1) * 128], lhsT=R_(Pq[nxt][:, i, :]), rhs=R_(Nn[:, i, :]), start=True, stop=True), reads=[('Pq', nxt), 'Nn'], writes=[kZ])
                            if lev < 6:
                                P.dve(lambda bZ=bZ: V.tensor_tensor(out=Nn[:], in0=bZ[:].rearrange("p (i f) -> p i f", f=128), in1=Nn[:], op=ALU.add), reads=[kZ, 'Nn'], writes=['Nn'])
                            else:
                                i0 = insts[0]
                                P.dve(lambda bZ=bZ, i0=i0: V.tensor_tensor(out=ATall[:, i0:i0 + 4, 384:512], in0=bZ[:].rearrange("p (i f) -> p i f", f=128), in1=Nn[:], op=ALU.add),
                                      reads=[kZ, 'Nn'], writes=[('ATq', i_) for i_ in insts])
                    cap_i = P.end_capture()
                    side = list(cap_a)
                    if d == 0:
                        P.capture()
                        prep_front(1, wb=(6, 7), ab=(6, 7))
                        cap_f = P.end_capture()
                        side = []
                        ia, jf = list(cap_a), list(cap_f)
                        while ia or jf:
                            for _ in range(3):
                                if ia:
                                    side.append(ia.pop(0))
                            if jf:
                                side.append(jf.pop(0))
                    P.replay_spread(cap_i, side)
                    if p == 0 and d == 0:
                        dbg_dump("ATall", ATall[:], [('AT', i_) for i_ in range(16)] + [('ATq', i_) for i_ in range(16)])
                    P.cut(8)
                    for e in range(2):
                        P.dma('sp', lambda p=p, d=d, e=e: nc.sync.dma_start(out=stin[64 * e:64 * e + 64, 64 * e:64 * e + 64], in_=st_d[d, 2 * p + e]), reads=['stin0'], writes=[('stin', e)], key=('stin', e))
                    pw = pb[6][:, 0:128]
                    pu = pb[6][:, 128:256]
                    py = pb[7][:, 0:128]
                    ph0 = pb[7][:, 128:256]
                    P.pe(lambda: nc.tensor.transpose(ph0, stin[:], ident_f[:]), reads=[('stin', 0), ('stin', 1), 'stin0', 'ident_f'], writes=['pb7'])
                    P.dve(lambda: V.tensor_copy(out=H32d[:], in_=ph0), reads=['pb7'], writes=['H32'])
                    P.act(lambda: A.copy(out=Hbd[:], in_=H32d[:]), reads=['H32'], writes=['Hb'])
                    P.cut(9)
                    order = list(range(8)) if d == 0 else list(range(7, -1, -1))
                    pend = []
                    if d == 1 and p + 1 < npairs:
                        pend = main_items(p + 1)
                    if _os.environ.get('K_ORDER'):
                        order = [int(t_) for t_ in _os.environ['K_ORDER'].split(',')]
                    pu2 = pb[4][:, 0:128]
                    ph = pb[5][:, 0:128]
                    for step, c in enumerate(order):
                        i0_, i1_ = c, 8 + c
                        P.pe(lambda c=c: nc.tensor.matmul(pw, lhsT=rat[:, c, 128:256], rhs=Hbd[:], start=True, stop=False), reads=['rat_a', 'Hb'], writes=['pb6'])
                        for e in range(2):
                            inst = e * 8 + c
                            P.pe(lambda c=c, e=e, inst=inst: nc.tensor.matmul(pw[:, e * 64:(e + 1) * 64], lhsT=ATall[:, inst, 128:256], rhs=VT[:, c, e * 64:(e + 1) * 64], start=False, stop=(e == 1)), reads=[('AT', inst), 'VT'], writes=['pb6'])
                        P.act(lambda: A.copy(out=Wsb[:], in_=pw), reads=['pb6'], writes=['Wsb'])
                        P.cut(9.1)
                        for e in range(2):
                            inst = e * 8 + c
                            P.pe(lambda e=e, inst=inst: nc.tensor.matmul(pu[:, e * 64:(e + 1) * 64], lhsT=ATall[:, inst, 384:512], rhs=Wsb[:, e * 64:(e + 1) * 64], start=True, stop=True), reads=[('ATq', inst), 'Wsb'], writes=['pb6'])
                        for e in range(2):
                            inst = e * 8 + c
                            P.pe(lambda e=e, inst=inst: nc.tensor.matmul(pu2[:, e * 64:(e + 1) * 64], lhsT=ATall[:, inst, 384:512], rhs=Wsb[:, e * 64:(e + 1) * 64], start=True, stop=True), reads=[('ATq', inst), 'Wsb'], writes=['pb4'])
                        P.dve(lambda: V.tensor_copy(out=Usb[:], in_=pu), reads=['pb6'], writes=['Usb'])
                        P.act(lambda: A.copy(out=Uz_diag, in_=pu2.rearrange("p (e i) -> p e i", e=2)), reads=['pb4'], writes=['Uz'])
                        P.cut(9.2)
                        P.pe(lambda c=c: nc.tensor.matmul(py, lhsT=Hbd[:], rhs=rat[:, c, 0:128], start=True, stop=False), reads=['Hb', 'rat_r'], writes=['pb7'])
                        for e in range(2):
                            inst = e * 8 + c
                            P.pe(lambda c=c, e=e, inst=inst: nc.tensor.matmul(py, lhsT=VTz[:, e, c, :], rhs=ATall[:, inst, 0:128], start=False, stop=False), reads=['VTz', ('AT', inst)], writes=['pb7'])
                        P.pe(lambda c=c: nc.tensor.matmul(ph, lhsT=KT[:, c, :], rhs=VT[:, c, :], start=True, stop=False), reads=['KT', 'VT'], writes=['pb5'])
                        P.pe(lambda c=c: nc.tensor.matmul(ph, lhsT=BT[:, c, :], rhs=Usb[:], start=False, stop=True), reads=['BT', 'Usb'], writes=['pb5'])
                        for e in range(2):
                            inst = e * 8 + c
                            P.pe(lambda c=c, e=e, inst=inst: nc.tensor.matmul(py, lhsT=Uz[:, e, :], rhs=ATall[:, inst, 256:384], start=False, stop=(e == 1)), reads=['Uz', ('AT', inst)], writes=['pb7'])
                        P.cut(9.4)
                        boundary = (step % 2 == 1) and not _os.environ.get('K_NOB')
                        if not boundary and step < 7:
                            for e in range(2):
                                r0 = 64 * e
                                P.dve(lambda c=c, d=d, r0=r0: V.scalar_tensor_tensor(out=Hbd[r0:r0 + 64, r0:r0 + 64], in0=H32d[r0:r0 + 64, r0:r0 + 64], scalar=gC[r0:r0 + 64, d, c:c + 1], in1=ph[r0:r0 + 64, r0:r0 + 64], op0=ALU.mult, op1=ALU.add), reads=['H32', ('gC', d), 'pb5'], writes=['Hb'])
                        for e in range(2):
                            r0 = 64 * e
                            P.dve(lambda c=c, d=d, r0=r0: V.scalar_tensor_tensor(out=H32d[r0:r0 + 64, r0:r0 + 64], in0=H32d[r0:r0 + 64, r0:r0 + 64], scalar=gC[r0:r0 + 64, d, c:c + 1], in1=ph[r0:r0 + 64, r0:r0 + 64], op0=ALU.mult, op1=ALU.add), reads=['H32', ('gC', d), 'pb5'], writes=['H32'])
                        if d == 0:
                            P.act(lambda c=c: A.copy(out=ysb[:, c * 128:(c + 1) * 128], in_=py), reads=['pb7'], writes=['ysb'])
                        else:
                            P.dve(lambda c=c: V.tensor_tensor(out=ysb[:, c * 128:(c + 1) * 128], in0=py, in1=ysb[:, c * 128:(c + 1) * 128], op=ALU.add), reads=['pb7', 'ysb'], writes=['ysb'])
                        P.cut(9.5)
                        if boundary:
                            q = c // 2
                            for e in range(2):
                                r0 = 64 * e
                                P.act(lambda d=d, q=q, r0=r0: A.copy(out=stst[r0:r0 + 64, d, q, :], in_=H32d[r0:r0 + 64, r0:r0 + 64]), reads=['H32'], writes=[('stst', d)])
                            if step < 7:
                                P.dve(lambda: V.tensor_scalar(out=H32d[:], in0=H32d[:], scalar1=C("flag"), scalar2=None, op0=ALU.mult), reads=['H32', 'cols'], writes=['H32'])
                                P.act(lambda: A.copy(out=Hbd[:], in_=H32d[:]), reads=['H32'], writes=['Hb'])
                        if pend:
                            for it in pend.pop(0):
                                it()
                    while pend:
                        for it in pend.pop(0):
                            it()
                    if d == 1 and p + 2 < npairs:
                        wks[p + 2] = pair_w(p + 2)
                    P.cut(10)
                    P.dma('sp', lambda p=p, d=d: nc.sync.dma_start(out=sto_d[d, :, p].rearrange("q r i -> r q i"), in_=stst[:, d, :, :]), reads=[('stst', d)], key=('sto', d))
                    if p == 0 and d == 0:
                        dbg_dump("ysb0", ysb[:], ['ysb'])
                P.cut(11)
                P.capture()
                if p == 0:
                    dbg_dump("ysb", ysb[:], ['ysb'])
                for tb in range(2):
                    P.pe(lambda tb=tb: nc.tensor.matmul(pb[tb][:], lhsT=bones[:], rhs=ysb[:, tb * 512:(tb + 1) * 512], start=True, stop=True), reads=['bones', 'ysb'], writes=['pb%d' % tb])
                    P.dve(lambda tb=tb: V.scalar_tensor_tensor(out=ft1[:, tb * 512:(tb + 1) * 512], in0=pb[tb][:], scalar=-1.0 / 64, in1=ysb[:, tb * 512:(tb + 1) * 512], op0=ALU.mult, op1=ALU.add), reads=['pb%d' % tb, 'ysb'], writes=['ft1'])
                P.act(lambda: A.activation(out=ft2[:], in_=ft1[:], func=AF.Square), reads=['ft1'], writes=['ft2'])
                for tb in range(2):
                    P.pe(lambda tb=tb: nc.tensor.matmul(pb[2 + tb][:], lhsT=bones[:], rhs=ft2[:, tb * 512:(tb + 1) * 512], start=True, stop=True), reads=['bones', 'ft2'], writes=['pb%d' % (2 + tb)])
                    P.dve(lambda tb=tb: V.tensor_scalar(out=fsig[:, tb * 512:(tb + 1) * 512], in0=pb[2 + tb][:], scalar1=1.0 / 64, scalar2=GN_EPS, op0=ALU.mult, op1=ALU.add), reads=['pb%d' % (2 + tb)], writes=['fsig'])
                P.act(lambda: A.activation(out=fsig[:], in_=fsig[:], func=AF.Sqrt), reads=['fsig'], writes=['fsig'])
                P.dve(lambda: V.reciprocal(out=fsig[:], in_=fsig[:]), reads=['fsig'], writes=['fsig'])
                P.dve(lambda: V.tensor_tensor(out=ft1[:], in0=ft1[:], in1=fsig[:], op=ALU.mult), reads=['ft1', 'fsig'], writes=['ft1'])
                P.pool(lambda p=p: G.tensor_scalar(out=ft1[:], in0=ft1[:], scalar1=C("lg", p), scalar2=C("lb", p), op0=ALU.mult, op1=ALU.add), reads=['ft1', 'cols'], writes=['ft1'])
                for tb in range(2):
                    P.pe(lambda tb=tb: nc.tensor.matmul(pb[tb][:], lhsT=bones[:], rhs=fs[:, tb * 512:(tb + 1) * 512], start=True, stop=True), reads=['bones', 'fs'], writes=['pb%d' % tb])
                    P.dve(lambda tb=tb, fvp=fvp: V.tensor_tensor(out=ft2[:, tb * 512:(tb + 1) * 512], in0=pb[tb][:], in1=fvp[:, tb * 512:(tb + 1) * 512], op=ALU.mult), reads=['pb%d' % tb, fvk], writes=['ft2'])
                P.dve(lambda: V.tensor_tensor(out=ft1[:], in0=ft1[:], in1=ft2[:], op=ALU.add), reads=['ft1', 'ft2'], writes=['ft1'])
                P.dve(lambda p=p, sgp=sgp: V.tensor_tensor(out=yA[:, p, :], in0=ft1[:], in1=sgp[:], op=ALU.mult), reads=['ft1', sgk], writes=[('yA', p)])
                if p == 0:
                    dbg_dump("yA0", yA[:, 0, :], [('yA', 0)])
                cap_post = P.end_capture()
                cap_sh = []
                if p + 1 < npairs:
                    P.capture()
                    emit_shared(p + 1)
                    cap_sh = P.end_capture()
                P.replay(cap_post, cap_sh)
            P.muted = False
            _pad = _os.environ.get('K_PAD')
            if _pad:
                kind, npad = _pad[0], int(_pad[1:])
                for i_ in range(npad):
                    if kind == 'A':
                        P.dve(lambda: nc.vector.tensor_copy(out=scr[0:1, 5:6], in_=scr[0:1, 4:5]), reads=[], writes=[])
                    elif kind == 'B':
                        P.dve(lambda: nc.vector.tensor_copy(out=scr[0:1, 5:6], in_=scr[0:1, 4:5]), reads=['padb'], writes=['pada'])
                        P.act(lambda: nc.scalar.copy(out=scr[0:1, 6:7], in_=scr[0:1, 5:6]), reads=['pada'], writes=['padb'])
                    elif kind == 'C':
                        P.dve(lambda: nc.vector.tensor_copy(out=Wsb[:], in_=Wsb[:]), reads=['pb6'], writes=['Wsb'])
                        P.pe(lambda: nc.tensor.matmul(pb[6][:, 0:128], lhsT=Wsb[:], rhs=Wsb[:], start=True, stop=True), reads=['Wsb'], writes=['pb6'])
            dbg_dump("yA", yA[:], [('yA', p) for p in range(16)])
        P.barrier(scr)
        YA_ALL = [('yA', p) for p in range(16)]

        if True:
            s2 = Arena(YA_END)
            yB = sb("yB", [128, KC, T], BF16, s2)
            wsl3 = sb("wsl3", [128, KC, 512], BF16, s2)
            slots = [(wslot, 'wslot'), (wsl2, 'wsl2')]
            slots3 = [(wslot, 'wslot'), (wsl2, 'wsl2'), (wsl3, 'wsl3')]
            items = ([('B', q_) for q_ in range(16 if upto >= 2 else 0)] + [('M', q_) for q_ in range(16 if upto >= 3 else 0)])
            wkeys = {}

            def issue(idx):
                kind, q_ = items[idx]
                slot_, key_ = slots3[idx % 3]
                if kind == 'B':
                    wkeys[idx] = load_w(slot_, key_, [(0, 512, winv[:, :, 8576:8576 + 8192].rearrange("p k (g f) -> p k g f", g=4)[:, :, :, q_ * 128:(q_ + 1) * 128])])
                else:
                    wkeys[idx] = load_w(slot_, key_, [
                        (0, 128, woav[:, :, q_ * 128:(q_ + 1) * 128]),
                        (128, 128, wobv[:, :, q_ * 128:(q_ + 1) * 128]),
                        (256, 256, winv[:, :, 16768:16768 + 4096].rearrange("p k (g f) -> p k g f", g=2)[:, :, :, q_ * 128:(q_ + 1) * 128])])

            def prefetch(idx):
                if idx == 0:
                    for j_ in range(min(2, len(items))):
                        issue(j_)
                if idx + 2 < len(items):
                    issue(idx + 2)
            cx = sb("cx", [128, T], F32, s2)
            cg = sb("cg", [128, T], F32, s2)
            u = sb("u", [128, T], F32, s2)
            sg2 = sb("sg2", [128, T], F32, s2)
            V = nc.vector
            G = nc.gpsimd
            A = nc.scalar
            for q in range(16 if upto >= 2 else 0):
                prefetch(q)
                slot, skey = slots3[q % 3]
                wk = wkeys[q]
                for tb in range(2):
                    ts_ = slice(tb * 512, (tb + 1) * 512)
                    hfn = lambda kc, tb=tb: hT[:, kc, tb * 512:(tb + 1) * 512]
                    b0 = (tb * 4)
                    for fi in range(4):
                        mm_group(pb[b0 + fi], 'pb%d' % (b0 + fi), slot, wk, fi * 128, 128, hfn, HT_ALL)
                    P.act(lambda ts_=ts_, b0=b0: A.copy(out=cg[:, ts_], in_=pb[b0 + 1][:]), reads=['pb%d' % (b0 + 1)], writes=['cg'])
                    P.dve(lambda ts_=ts_, b0=b0: V.tensor_tensor(out=cx[:, ts_], in0=pb[b0 + 2][:], in1=cg[:, ts_], op=ALU.mult), reads=['pb%d' % (b0 + 2), 'cg'], writes=['cx'])
                    P.act(lambda ts_=ts_, b0=b0: A.activation(out=sg2[:, ts_], in_=pb[b0 + 3][:], func=AF.Silu), reads=['pb%d' % (b0 + 3)], writes=['sg2'])
                    P.dve(lambda ts_=ts_, b0=b0: V.tensor_tensor(out=sg2[:, ts_], in0=pb[b0][:], in1=sg2[:, ts_], op=ALU.mult), reads=['pb%d' % b0, 'sg2'], writes=['sg2'])
                cx3 = cx[:].rearrange("p (r t) -> p r t", t=64)
                u3 = u[:].rearrange("p (r t) -> p r t", t=64)
                P.act(lambda q=q: A.activation(out=u[:], in_=cx[:], func=AF.Copy, scale=C("cw", 16 + q)), reads=['cx', 'cols'], writes=['u'])
                P.dve(lambda q=q: V.scalar_tensor_tensor(out=u3[:, :, 1:64], in0=cx3[:, :, 0:63], scalar=C("cw", q), in1=u3[:, :, 1:64], op0=ALU.mult, op1=ALU.add), reads=['cx', 'cols', 'u'], writes=['u'])
                P.dve(lambda q=q: V.scalar_tensor_tensor(out=u3[:, :, 0:63], in0=cx3[:, :, 1:64], scalar=C("cw", 32 + q), in1=u3[:, :, 0:63], op0=ALU.mult, op1=ALU.add), reads=['cx', 'cols', 'u'], writes=['u'])
                cx4 = cx[:].rearrange("p (s r t) -> p s r t", s=4, t=64)
                u4 = u[:].rearrange("p (s r t) -> p s r t", s=4, t=64)
                P.dve(lambda q=q: V.scalar_tensor_tensor(out=u4[:, :, 1:4, 0], in0=cx4[:, :, 0:3, 63], scalar=DC("cw0n", q), in1=u4[:, :, 1:4, 0], op0=ALU.mult, op1=ALU.add), reads=['cx', 'dcols', 'u'], writes=['u'])
                P.dve(lambda q=q: V.scalar_tensor_tensor(out=u4[:, :, 0:3, 63], in0=cx4[:, :, 1:4, 0], scalar=DC("cw2n", q), in1=u4[:, :, 0:3, 63], op0=ALU.mult, op1=ALU.add), reads=['cx', 'dcols', 'u'], writes=['u'])
                P.pool(lambda q=q: G.tensor_tensor(out=yB[:, q, :], in0=u[:], in1=sg2[:], op=ALU.mult), reads=['u', 'sg2'], writes=[('yB', q)])
            dbg_dump("yB", yB[:], [('yB', q) for q in range(16)])
            YB_ALL = [('yB', q) for q in range(16)]

            merged = Rbf
            sa = sb("sa", [128, 2, 512], F32, s2)
            sbm = sb("sbm", [128, 2, 512], F32, s2)
            print("phase2/3a arena end", s2.off, "of", ARENA)
            for q in range(16 if upto >= 3 else 0):
                idx_ = (16 if upto >= 2 else 0) + q
                prefetch(idx_)
                slot, skey = slots3[idx_ % 3]
                wk = wkeys[idx_]
                for tb in range(2):
                    ts_ = slice(tb * 512, (tb + 1) * 512)
                    b0 = tb * 4
                    mm_group(pb[b0], 'pb%d' % b0, slot, wk, 0, 128, lambda kc, ts_=ts_: yA[:, kc, ts_], YA_ALL)
                    mm_group(pb[b0 + 1], 'pb%d' % (b0 + 1), slot, wk, 128, 128, lambda kc, ts_=ts_: yB[:, kc, ts_], YB_ALL)
                    mm_group(pb[b0 + 2], 'pb%d' % (b0 + 2), slot, wk, 256, 128, lambda kc, ts_=ts_: hT[:, kc, ts_], HT_ALL)
                    mm_group(pb[b0 + 3], 'pb%d' % (b0 + 3), slot, wk, 384, 128, lambda kc, ts_=ts_: hT[:, kc, ts_], HT_ALL)
                    P.act(lambda tb=tb, b0=b0: A.activation(out=sa[:, tb, :], in_=pb[b0 + 2][:], func=AF.Sigmoid), reads=['pb%d' % (b0 + 2)], writes=[('sa', tb)])
                    P.act(lambda tb=tb, b0=b0: A.activation(out=sbm[:, tb, :], in_=pb[b0 + 3][:], func=AF.Sigmoid), reads=['pb%d' % (b0 + 3)], writes=[('sbm', tb)])
                    P.dve(lambda tb=tb, b0=b0: V.tensor_tensor(out=sa[:, tb, :], in0=pb[b0][:], in1=sa[:, tb, :], op=ALU.mult), reads=['pb%d' % b0, ('sa', tb)], writes=[('sa', tb)])
                    P.dve(lambda tb=tb, b0=b0: V.tensor_tensor(out=sbm[:, tb, :], in0=pb[b0 + 1][:], in1=sbm[:, tb, :], op=ALU.mult), reads=['pb%d' % (b0 + 1), ('sbm', tb)], writes=[('sbm', tb)])
                    P.pool(lambda tb=tb, q=q, ts_=ts_: G.tensor_tensor(out=merged[:, q, ts_], in0=sa[:, tb, :], in1=sbm[:, tb, :], op=ALU.add), reads=[('sa', tb), ('sbm', tb)], writes=[('merged', q)])
            dbg_dump("merged", merged[:], [('merged', q) for q in range(16)])
            MG_ALL = [('merged', q) for q in range(16)]

            P.barrier(scr)
            s3 = Arena(HT_OFF)
            xnew = sb("xnew", [128, 8, D], F32, s3)
            fgb = sb("fgb", [128, D], F32, s3)
            xin = [sb("xin%d" % i, [128, 512], F32, s3) for i in range(4)]
            osq = sb("osq", [128, D], F32, s3)
            ost = sb("ost", [128, 8, 4], F32, s3)
            gate_bc = sb("gate_bc", [128, D], F32, s3)
            rowsb = sb("rowsb", [128, D], F32, s3)
            screp = sb("screp", [128, KC, 128], F32, s3)
            aslot3 = sb("aslot3", [128, KC, 256], F32, s3)
            print("phase3b arena end", s3.off, "of", ARENA)
            P.dma('sp', lambda: nc.sync.dma_start(out=fgb[:], in_=rows_d[:, 1, :]), writes=['fgb'], key='c_fgb')
            P.dma('sp', lambda: nc.sync.dma_start(out=rowsb[:], in_=rows_d[:, 0, :]), writes=['rowsb'], key='c_rows')
            P.dve(lambda: V.tensor_copy(out=screp[:], in_=sc2[:, :, 0:1].to_broadcast([128, KC, 128])), reads=['sc2a'], writes=['screp'])
            asl2 = aslot3[:].rearrange("p k f -> p (k f)")
            for kc in range(KC if upto >= 4 else 0):
                sk = ('aslot3', kc % 2)
                asl = asl2[:, (kc % 2) * 2048:(kc % 2 + 1) * 2048]
                P.dma('sp', lambda kc=kc, asl=asl: nc.sync.dma_start(out=asl, in_=adaw_d[kc * 128:(kc + 1) * 128, 4096:6144]), writes=[sk], key=sk)
                for blk in range(4):
                    P.pe(lambda kc=kc, blk=blk, asl=asl: nc.tensor.matmul(pb[blk][:], lhsT=screp[:, kc, :], rhs=asl[:, blk * 512:(blk + 1) * 512], start=(kc == 0), stop=(kc == KC - 1)),
                         reads=[sk, 'screp'], writes=['pb%d' % blk])
            for blk in range(4 if upto >= 4 else 0):
                P.dve(lambda blk=blk: V.tensor_tensor(out=gate_bc[:, blk * 512:(blk + 1) * 512], in0=pb[blk][:], in1=rowsb[:, blk * 512:(blk + 1) * 512], op=ALU.add),
                      reads=['pb%d' % blk, 'rowsb'], writes=[('gate_bc', blk)])
            for nb in range(4 if upto >= 4 else 0):
                slot, skey = slots[nb % 2]
                ns_ = slice(nb * 512, (nb + 1) * 512)
                wk = load_w(slot, skey, [(0, 512, wov[:, :, ns_])])
                for tt in range(8):
                    bank = pb[tt]
                    bk = 'pb%d' % tt
                    for kc in range(KC):
                        P.pe(lambda kc=kc, tt=tt, bank=bank, slot=slot: nc.tensor.matmul(bank[:], lhsT=merged[:, kc, tt * 128:(tt + 1) * 128], rhs=slot[:, kc, :], start=(kc == 0), stop=(kc == KC - 1)), reads=wk + MG_ALL, writes=[bk])
                    xi = xin[tt % 4]
                    xk = ('xin', tt % 4)
                    P.dma('sp', lambda tt=tt, ns_=ns_, xi=xi: nc.sync.dma_start(out=xi[:], in_=x_d[tt * 128:(tt + 1) * 128, ns_]), writes=[xk], key=xk)
                    P.dve(lambda tt=tt, ns_=ns_, bank=bank: V.tensor_tensor(out=xnew[:, tt, ns_], in0=bank[:], in1=gate_bc[:, ns_], op=ALU.mult), reads=[bk] + [('gate_bc', i) for i in range(4)], writes=[('xnew', tt, nb)])
                    P.pool(lambda tt=tt, ns_=ns_, xi=xi: G.tensor_tensor(out=xnew[:, tt, ns_], in0=xnew[:, tt, ns_], in1=xi[:], op=ALU.add), reads=[('xnew', tt, nb), xk], writes=[('xnew', tt, nb)])
            for tt in range(8 if upto >= 4 else 0):
                XK = [('xnew', tt, nb) for nb in range(4)]
                P.act(lambda tt=tt: A.activation(out=osq[:], in_=xnew[:, tt, :], func=AF.Square), reads=XK, writes=['osq'])
                P.dve(lambda tt=tt: V.tensor_reduce(out=ost[:, tt, 0:1], in_=osq[:], axis=AX.X, op=ALU.add), reads=['osq'], writes=[('ost', tt)])
                P.dve(lambda tt=tt: V.tensor_scalar(out=ost[:, tt, 1:2], in0=ost[:, tt, 0:1], scalar1=1.0 / D, scalar2=RMS_EPS, op0=ALU.mult, op1=ALU.add), reads=[('ost', tt)], writes=[('ost', tt)])
                P.act(lambda tt=tt: A.activation(out=ost[:, tt, 2:3], in_=ost[:, tt, 1:2], func=AF.Sqrt), reads=[('ost', tt)], writes=[('ost', tt)])
                P.dve(lambda tt=tt: V.reciprocal(out=ost[:, tt, 3:4], in_=ost[:, tt, 2:3]), reads=[('ost', tt)], writes=[('ost', tt)])
                P.dve(lambda tt=tt: V.scalar_tensor_tensor(out=xnew[:, tt, :], in0=xnew[:, tt, :], scalar=ost[:, tt, 3:4], in1=fgb[:], op0=ALU.mult, op1=ALU.mult), reads=XK + [('ost', tt), 'fgb'], writes=[('xo', tt)])
                P.dma('sp', lambda tt=tt: nc.sync.dma_start(out=y_d[tt * 128:(tt + 1) * 128, :], in_=xnew[:, tt, :]), reads=[('xo', tt)], key=('yout', tt % 2))
        P.emit()
        stats = P.stats
    return nc, stats


def _col(v):
    v = np.asarray(v, np.float32).reshape(-1, 128)
    return np.ascontiguousarray(v.T)


def _col96(v):
    v = np.asarray(v, np.float32).reshape(4, 96)
    out = np.zeros((128, 4), np.float32)
    out[:96, :] = v.T
    return out


def _host_constants():
    s = np.arange(128)[:, None]
    t = np.arange(128)[None, :]
    incl_f = (t >= s).astype(np.float32)
    strict_f = (t > s).astype(np.float32)
    incl_b = (t <= s).astype(np.float32)
    strict_b = (t < s).astype(np.float32)
    mask4 = np.stack([np.concatenate([incl_f, strict_f, incl_f, strict_f], 1),
                      np.concatenate([incl_b, strict_b, incl_b, strict_b], 1)], 0)
    maskP = np.stack([np.tile(strict_b, (1, 4)), np.tile(strict_f, (1, 4))], 0)
    cm = np.ones((128, T), np.float32)
    cm[:, 0::128] = 0.0
    ident = np.eye(128, dtype=np.float32)
    bones = np.kron(np.eye(2, dtype=np.float32), np.ones((64, 64), np.float32))
    return dict(mask4=mask4.astype(np.float32), maskP=maskP.astype(np.float32), cm=cm, ident=ident, bones=bones)


_CACHE = {}


def _get_program(dbg=()):
    key = tuple(dbg)
    if key not in _CACHE:
        _CACHE[key] = build_program(dbg)
    return _CACHE[key]


def make_in_maps(inp):
    f = lambda a: np.ascontiguousarray(np.asarray(a, np.float32))
    x_prompt = f(inp["x_prompt"])
    x_sample = f(inp["x_sample"])
    c = f(inp["c"])
    c_ctx = f(inp["c_ctx"])
    mu_prev = f(inp["mu_prev"])[0]
    mu_next = f(inp["mu_next"])[0]
    ada_b = f(inp["ada_b"])[0]
    consts = _host_constants()
    base_cols = [
        _col(mu_prev[:6144]), _col(mu_next[:6144]), _col96(mu_prev[6144:]), _col96(mu_next[6144:]),
        _col(f(inp["w0"])[0].reshape(-1)), _col(f(inp["a0"])[0].reshape(-1)),
        _col(f(inp["k_k"])[0]), _col(f(inp["k_a"])[0]), _col(f(inp["r_k"])[0].reshape(-1)),
        _col(f(inp["lnx_g"])[0]), _col(f(inp["lnx_b"])[0]), _col(f(inp["conv_w"])[0].reshape(-1)),
        _col(f(inp["norm_g"])[0]), _col(ada_b[0:D]), _col(ada_b[D:2 * D]),
    ]
    rows = np.stack([np.broadcast_to(ada_b[2 * D:], (128, D)), np.broadcast_to(f(inp["final_g"]), (128, D))], 1)
    rows = np.ascontiguousarray(rows, np.float32)
    shared = dict(
        ada_w=f(inp["ada_w"])[0], w_in=f(inp["w_in"])[0], w2=f(inp["w2"])[0], a2=f(inp["a2"])[0],
        w_out_a=f(inp["w_out_a"])[0], w_out_b=f(inp["w_out_b"])[0], w_o=f(inp["w_o"])[0],
        rows=rows, **consts)
    sf = f(inp["state_wkv_fwd"])
    sbw = f(inp["state_wkv_bwd"])
    in_maps = []
    for core in range(8):
        if core < 4:
            x = x_sample[core]
            cv = c[core]
            flag = 1.0
            stt = np.stack([sf[core, 0], sbw[core, 0]], 0)
        else:
            x = x_prompt[4 * (core - 4):4 * (core - 3)].reshape(T, D)
            cv = c_ctx
            flag = 0.0
            stt = np.zeros((2, NH, 64, 64), np.float32)
        cols = np.concatenate(base_cols + [np.full((128, 1), flag, np.float32)], 1)
        assert cols.shape[1] == NCOLS
        m = dict(shared)
        m.update(x=np.ascontiguousarray(x), cvec=_col(cv), cols=np.ascontiguousarray(cols), st=np.ascontiguousarray(stt))
        in_maps.append(m)
    return in_maps


def kernel(**inp):
    nc, _ = _get_program()
    in_maps = make_in_maps(inp)
    res = run_bass_kernel_spmd(nc, in_maps, core_ids=list(range(8)))
    r = res.results
    y_sample = np.stack([r[i]["y"] for i in range(4)], 0).astype(np.float32)
    y_prompt = np.concatenate([r[i]["y"].reshape(4, 256, D) for i in range(4, 8)], 0).astype(np.float32)
    def st_out(d):
        a = np.concatenate([r[i]["sto"][d] for i in range(4, 8)], 0)
        a = a.reshape(16, 16, 2, 64, 64).transpose(0, 1, 2, 4, 3)
        return np.ascontiguousarray(a.reshape(16, 1, NH, 64, 64).astype(np.float32))
    nf = st_out(0)
    nb = st_out(1)
    return (y_prompt, y_sample, nf, nb)
```
